# Optimizing a Trainium2 kernel written in Bass

```python
import math
import jax, jax.numpy as jnp
from jax import lax
import numpy as np

D_MODEL = 1024
BATCH = 2
SEQ = 8192
DEPTH = 4

N_EVEN = (DEPTH + 1) // 2
N_ODD = DEPTH // 2
D_FF = 2816
N_MOD = 9
EPS = 1e-6

ATT_HEADS = 4
ATT_QK_DIM = 64
ATT_V_DIM = 2 * ATT_QK_DIM
QK_COLS = ATT_HEADS * 2 * ATT_QK_DIM
ATT_WIDTH = ATT_HEADS * ATT_V_DIM
Q_BLOCK = 128

POOL_WINDOWS = (2, 4, 8, 16)
POOL_GROUPS = len(POOL_WINDOWS)
POOL_WIDTH = D_MODEL // 2
POOL_GROUP_DIM = POOL_WIDTH // POOL_GROUPS

AB_IN_WIDTH = 2 * QK_COLS + ATT_WIDTH + POOL_WIDTH
MIX_WIDTH = ATT_WIDTH + POOL_WIDTH

CONV_WIDTH = D_MODEL
CONV_K = 3

kernel_name = "hybrid_diffattn_pool_shortconv_macaron_adaln"


def rmsnorm(x, g):
    xf = x.astype(jnp.float32)
    y = xf * lax.rsqrt(jnp.mean(xf * xf, axis=-1, keepdims=True) + EPS)
    return (y * g.astype(jnp.float32)).astype(x.dtype)


def modulated_norm(x, g, shift, scale):
    return rmsnorm(x, g) * (1 + scale) + shift


def swiglu(h, wg, wu, wd):
    return (jax.nn.silu(h @ wg) * (h @ wu)) @ wd


def diff_attention(q1, q2, k1, k2, v, lam):
    b, h, s, _ = q1.shape
    nb = s // Q_BLOCK
    scale = ATT_QK_DIM ** -0.5
    kpos = jnp.arange(s)

    def blocks(q):
        return q.reshape(b, h, nb, Q_BLOCK, q.shape[-1]).transpose(2, 0, 1, 3, 4)

    def one_block(args):
        qa, qb, i = args
        qpos = i * Q_BLOCK + jnp.arange(Q_BLOCK)
        mask = kpos[None, :] <= qpos[:, None]

        def probs(q, k):
            sc = jnp.einsum('bhqd,bhkd->bhqk', q, k).astype(jnp.float32) * scale
            sc = jnp.where(mask, sc, -jnp.inf)
            return jax.nn.softmax(sc, axis=-1)

        w = probs(qa, k1) - lam * probs(qb, k2)
        return jnp.einsum('bhqk,bhkd->bhqd', w.astype(v.dtype), v)

    out = lax.map(one_block, (blocks(q1), blocks(q2), jnp.arange(nb)))
    return out.transpose(1, 2, 0, 3, 4).reshape(b, h, s, v.shape[-1])


def multiscale_pool(u, pool_w, pool_scale):
    b, s, _ = u.shape
    uf = u.astype(jnp.float32).reshape(b, s, POOL_GROUPS, POOL_GROUP_DIM)
    cs = jnp.cumsum(uf, axis=1)
    t = jnp.arange(s)
    outs = []
    for g, w in enumerate(POOL_WINDOWS):
        csg = cs[:, :, g]
        lagged = jnp.pad(csg, ((0, 0), (w, 0), (0, 0)))[:, :s]
        cnt = jnp.minimum(t + 1, w).astype(jnp.float32)[None, :, None]
        outs.append((csg - lagged) / cnt - uf[:, :, g])
    d = jnp.stack(outs, axis=2)
    y = jnp.einsum('bsgc,gcd->bsgd', d, pool_w.astype(jnp.float32))
    y = y.reshape(b, s, POOL_WIDTH) * pool_scale.astype(jnp.float32)
    return y.astype(u.dtype)


def attn_pool_mixer(h, layer_idx, w_in, qk_g, lq1, lk1, lq2, lk2, subln_g, pool_w, pool_scale, w_out):
    b, s, _ = h.shape
    proj = h @ w_in
    q, k, v, u = jnp.split(proj, [QK_COLS, 2 * QK_COLS, 2 * QK_COLS + ATT_WIDTH], axis=-1)
    q = rmsnorm(q.reshape(b, s, ATT_HEADS, 2, ATT_QK_DIM), qk_g[0]).transpose(0, 2, 3, 1, 4)
    k = rmsnorm(k.reshape(b, s, ATT_HEADS, 2, ATT_QK_DIM), qk_g[1]).transpose(0, 2, 3, 1, 4)
    v = v.reshape(b, s, ATT_HEADS, ATT_V_DIM).transpose(0, 2, 1, 3)
    lam_init = 0.8 - 0.6 * math.exp(-0.3 * layer_idx)
    f32 = jnp.float32
    lam = (jnp.exp(jnp.sum(lq1.astype(f32) * lk1.astype(f32)))
           - jnp.exp(jnp.sum(lq2.astype(f32) * lk2.astype(f32))) + lam_init)
    o = diff_attention(q[:, :, 0], q[:, :, 1], k[:, :, 0], k[:, :, 1], v, lam)
    o = rmsnorm(o, subln_g) * (1 - lam_init)
    o = o.transpose(0, 2, 1, 3).reshape(b, s, ATT_WIDTH)
    p = multiscale_pool(u, pool_w, pool_scale)
    return jnp.concatenate([o, p], axis=-1) @ w_out


def short_conv_mixer(h, w_in, conv_w, w_out):
    b_gate, c_gate, xin = jnp.split(h @ w_in, 3, axis=-1)
    v = c_gate * xin
    s = v.shape[1]
    vp = jnp.pad(v, ((0, 0), (CONV_K - 1, 0), (0, 0)))
    y = conv_w[0] * vp[:, 0:s]
    for j in range(1, CONV_K):
        y = y + conv_w[j] * vp[:, j:j + s]
    return (b_gate * y) @ w_out


def setup_inputs(seed: int = 0) -> dict:
    key = jax.random.key(seed)
    ks = jax.random.split(key, 24)
    f32 = jnp.float32

    def nrm(k, shape, scale):
        return jax.random.normal(k, shape, f32) * scale

    D = D_MODEL
    return {
        "x": nrm(ks[0], (BATCH, SEQ, D), 1.0),
        "c": nrm(ks[1], (BATCH, D), 1.0),
        "norm_g": 1.0 + nrm(ks[2], (DEPTH, 3, D), 0.05),
        "w_ada": nrm(ks[3], (DEPTH, D, N_MOD * D), D ** -0.5),
        "b_ada": nrm(ks[4], (DEPTH, N_MOD * D), 0.02),
        "ffn_wg": nrm(ks[5], (DEPTH, 2, D, D_FF), D ** -0.5),
        "ffn_wu": nrm(ks[6], (DEPTH, 2, D, D_FF), D ** -0.5),
        "ffn_wd": nrm(ks[7], (DEPTH, 2, D_FF, D), D_FF ** -0.5),
        "w_in_ab": nrm(ks[8], (N_EVEN, D, AB_IN_WIDTH), D ** -0.5),
        "qk_norm_g": 1.0 + nrm(ks[9], (N_EVEN, 2, ATT_QK_DIM), 0.05),
        "lambda_q1": nrm(ks[10], (N_EVEN, ATT_QK_DIM), 0.1),
        "lambda_k1": nrm(ks[11], (N_EVEN, ATT_QK_DIM), 0.1),
        "lambda_q2": nrm(ks[12], (N_EVEN, ATT_QK_DIM), 0.1),
        "lambda_k2": nrm(ks[13], (N_EVEN, ATT_QK_DIM), 0.1),
        "subln_g": 1.0 + nrm(ks[14], (N_EVEN, ATT_V_DIM), 0.05),
        "pool_w": nrm(ks[15], (N_EVEN, POOL_GROUPS, POOL_GROUP_DIM, POOL_GROUP_DIM), POOL_GROUP_DIM ** -0.5),
        "pool_scale": 1.0 + nrm(ks[16], (N_EVEN, POOL_WIDTH), 0.1),
        "w_out_ab": nrm(ks[17], (N_EVEN, MIX_WIDTH, D), MIX_WIDTH ** -0.5),
        "w_in_c": nrm(ks[18], (N_ODD, D, 3 * CONV_WIDTH), D ** -0.5),
        "conv_w": nrm(ks[19], (N_ODD, CONV_K, CONV_WIDTH), CONV_K ** -0.5),
        "w_out_c": nrm(ks[20], (N_ODD, CONV_WIDTH, D), CONV_WIDTH ** -0.5),
    }


def reference(x, c, norm_g, w_ada, b_ada, ffn_wg, ffn_wu, ffn_wd, w_in_ab, qk_norm_g,
              lambda_q1, lambda_k1, lambda_q2, lambda_k2, subln_g, pool_w, pool_scale,
              w_out_ab, w_in_c, conv_w, w_out_c):
    b = x.shape[0]
    c_act = jax.nn.silu(c)
    for l in range(DEPTH):
        mod = (c_act @ w_ada[l] + b_ada[l]).reshape(b, N_MOD, 1, D_MODEL)
        sh1, sc1, g1, sh2, sc2, g2, sh3, sc3, g3 = [mod[:, i] for i in range(N_MOD)]
        h = modulated_norm(x, norm_g[l, 0], sh1, sc1)
        x = x + 0.5 * g1 * swiglu(h, ffn_wg[l, 0], ffn_wu[l, 0], ffn_wd[l, 0])
        h = modulated_norm(x, norm_g[l, 1], sh2, sc2)
        if l % 2 == 0:
            e = l // 2
            m = attn_pool_mixer(h, l, w_in_ab[e], qk_norm_g[e], lambda_q1[e], lambda_k1[e],
                                lambda_q2[e], lambda_k2[e], subln_g[e], pool_w[e],
                                pool_scale[e], w_out_ab[e])
        else:
            o = l // 2
            m = short_conv_mixer(h, w_in_c[o], conv_w[o], w_out_c[o])
        x = x + g2 * m
        h = modulated_norm(x, norm_g[l, 2], sh3, sc3)
        x = x + 0.5 * g3 * swiglu(h, ffn_wg[l, 1], ffn_wu[l, 1], ffn_wd[l, 1])
    return x
```

```python
import numpy as np
import concourse.bass as bass
import concourse.mybir as mybir
from concourse.bass_utils import run_bass_kernel_spmd

F32 = mybir.dt.float32
BF16 = mybir.dt.bfloat16
ACT = mybir.ActivationFunctionType
ALU = mybir.AluOpType


import contextlib
import ml_dtypes

N_CORES = 8
D_MODEL, BATCH, SEQ, DEPTH = 1024, 2, 8192, 4
D_FF = 2816
NF = D_FF // 128
TOK = SEQ * BATCH // N_CORES
HALO = 64
NTOK = TOK + HALO
TT = 352
NT = NTOK // TT
EPS = 1e-6
LAM_INIT = {0: 0.8 - 0.6 * float(np.exp(-0.3 * 0)), 2: 0.8 - 0.6 * float(np.exp(-0.3 * 2))}
POOL_W = (2, 4, 8, 16)


class Prog:
    def __init__(self, nc, es):
        self.nc, self.es = nc, es
        self.eng = dict(pe=nc.tensor, act=nc.scalar, dve=nc.vector, pool=nc.gpsimd, sp=nc.sync)
        self.rec = {k: [] for k in self.eng}
        self.sem = {k: es.enter_context(nc.semaphore("s_" + k)) for k in ("pe", "act", "dve", "pool")}
        self.cnt = {k: 0 for k in self.sem}
        self.dsem = {}
        self.waited = {k: {} for k in self.eng}
        self.lastw, self.rd = {}, {}

    def _deps(self, e, reads, writes, nowaw=False):
        toks = []
        for r in reads:
            t = self.lastw.get(r)
            if t:
                toks.append(t)
        for w in writes:
            t = self.lastw.get(w)
            if t and not nowaw:
                toks.append(t)
            toks.extend(self.rd.get(w, {}).values())
        waits = {}
        for (s, v, te) in toks:
            if te == "pe" and e == "pe":
                continue
            k = id(s)
            if self.waited[e].get(k, 0) >= v:
                continue
            if k not in waits or waits[k][1] < v:
                waits[k] = (s, v)
        for k, (s, v) in waits.items():
            self.waited[e][k] = v
        return list(waits.values())

    def _commit(self, tok, reads, writes):
        for w in writes:
            self.lastw[w] = tok
            self.rd[w] = {}
        for r in reads:
            d = self.rd.setdefault(r, {})
            k = id(tok[0])
            if k not in d or d[k][1] < tok[1]:
                d[k] = tok

    def op(self, e, fn, reads=(), writes=()):
        waits = self._deps(e, reads, writes)
        self.cnt[e] += 1
        tok = (self.sem[e], self.cnt[e], e)
        self.rec[e].append((waits, fn, (self.sem[e], 1)))
        self._commit(tok, reads, writes)

    def mm(self, fns, reads=(), writes=()):
        waits = self._deps("pe", reads, writes)
        self.cnt["pe"] += 1
        tok = (self.sem["pe"], self.cnt["pe"], "pe")
        n = len(fns)
        for i, fn in enumerate(fns):
            self.rec["pe"].append((waits if i == 0 else [], fn, (self.sem["pe"], 1) if i == n - 1 else None))
        self._commit(tok, reads, writes)

    def dma(self, q, fn, key, reads=(), writes=(), nowaw=False):
        if key not in self.dsem:
            self.dsem[key] = [self.es.enter_context(self.nc.semaphore("d%d" % len(self.dsem))), 0]
        waits = self._deps(q, reads, writes, nowaw)
        d = self.dsem[key]
        d[1] += 16
        tok = (d[0], d[1], None)
        self.rec[q].append((waits, fn, (d[0], 16)))
        self._commit(tok, reads, writes)

    def barrier(self):
        for e in self.eng:
            waits = []
            for k, s in self.sem.items():
                if k != e and self.cnt[k] > self.waited[e].get(id(s), 0):
                    waits.append((s, self.cnt[k]))
                    self.waited[e][id(s)] = self.cnt[k]
                if k == e and e != "pe" and self.cnt[k] > self.waited[e].get(id(s), 0):
                    waits.append((s, self.cnt[k]))
                    self.waited[e][id(s)] = self.cnt[k]
            for d in self.dsem.values():
                if d[1] > self.waited[e].get(id(d[0]), 0):
                    waits.append((d[0], d[1]))
                    self.waited[e][id(d[0])] = d[1]
            if waits:
                self.rec[e].append((waits, None, None))
        self.lastw, self.rd = {}, {}

    def emit(self):
        self.barrier()
        with self.nc.Block() as block:
            for name, dec in (("pe", block.tensor), ("act", block.scalar), ("dve", block.vector),
                              ("pool", block.gpsimd), ("sp", block.sync)):
                recs = self.rec[name]

                def body(eng, recs=recs):
                    for waits, fn, inc in recs:
                        for s, v in waits:
                            eng.wait_ge(s, v)
                        if fn is None:
                            continue
                        ins = fn(eng)
                        if inc is not None:
                            ins.then_inc(inc[0], inc[1])
                dec(body)


class Arena:
    def __init__(self, nc, es, words):
        self.t = es.enter_context(nc.sbuf_tensor("arena", [128, words], F32))
        self.words, self.off = words, 0

    def mark(self):
        return self.off

    def reset(self, m):
        self.off = m

    def alloc(self, shape_free, dtype):
        n = int(np.prod(shape_free))
        w = n if dtype == F32 else (n + 1) // 2
        w = (w + 7) // 8 * 8
        assert self.off + w <= self.words, ("SBUF arena overflow", self.off, w, self.words)
        ap = self.t[:, self.off:self.off + w]
        self.off += w
        if dtype != F32:
            ap = ap.bitcast(dtype)
        ap = ap[:, 0:n]
        if len(shape_free) == 2:
            ap = ap.rearrange("p (a b) -> p a b", a=shape_free[0])
        elif len(shape_free) == 3:
            ap = ap.rearrange("p (a b c) -> p a b c", a=shape_free[0], b=shape_free[1])
        return ap


def tsl(t):
    return slice(t * TT, (t + 1) * TT)


def build_T(stages, first):
    nc = bass.Bass("TRN2", target_bir_lowering=False)
    es = contextlib.ExitStack()
    p = Prog(nc, es)
    ar = Arena(nc, es, 53000)
    ps = [es.enter_context(nc.psum_tensor("ps%d" % i, [128, 512], F32)) for i in range(8)]
    names_in, names_out = [], []

    def din(name, shape, dt=F32):
        names_in.append(name)
        return nc.dram_tensor(name, list(shape), dt, kind="ExternalInput").ap()

    def dout(name, shape, dt=F32):
        names_out.append(name)
        return nc.dram_tensor(name, list(shape), dt, kind="ExternalOutput").ap()

    layers = sorted({s[1] for s in stages})
    xT = ar.alloc([8, NTOK], F32)
    ones_bf = ar.alloc([128], BF16)
    bd_bf = ar.alloc([128], BF16)
    valid = ar.alloc([TT], F32)
    wA = [ar.alloc([8, 128], BF16) for _ in range(3)]
    wB = [ar.alloc([8, 128], BF16) for _ in range(3)]
    wD = [ar.alloc([NF, 128], BF16) for _ in range(3)]
    sq = [ar.alloc([8, TT], BF16) for _ in range(2)]
    tA = [ar.alloc([TT], F32) for _ in range(2)]
    rstd = [ar.alloc([TT], F32) for _ in range(2)]
    tB = [ar.alloc([TT], F32) for _ in range(4)]
    sg = [ar.alloc([TT], F32) for _ in range(2)]
    stg = sg
    cT = ar.alloc([8], F32)
    cact = ar.alloc([8], BF16)
    M = {}
    for l in layers:
        M[l] = dict(mod=ar.alloc([72], F32), bada=ar.alloc([72], F32), ng=ar.alloc([24], F32),
                    gs=[ar.alloc([8], F32) for _ in range(3)], gd=[ar.alloc([8], F32) for _ in range(3)])
    phase_mark = ar.mark()

    x_in = din("x_in", [D_MODEL, NTOK])
    valid_in = din("valid", [128, TT])
    c_in = din("cT", [128, 8])

    p.op("dve", lambda e: e.memset(ones_bf, 1.0), writes=["ones"])
    p.op("dve", lambda e: e.memset(bd_bf, 0.0), writes=["bd"])
    p.op("dve", lambda e: e.memset(bd_bf[0:64, 0:64], 1.0), writes=["bd"])
    p.op("dve", lambda e: e.memset(bd_bf[64:128, 64:128], 1.0), writes=["bd"])
    p.dma("sp", lambda e: e.dma_start(out=xT, in_=x_in.rearrange("(c q) t -> q c t", q=128)), "x",
          writes=[("xT", c, t) for c in range(8) for t in range(NT)])
    p.dma("sp", lambda e: e.dma_start(out=valid, in_=valid_in), "cst", writes=["valid"])
    p.dma("sp", lambda e: e.dma_start(out=cT, in_=c_in), "cst2", writes=["cT"])
    p.op("act", lambda e: e.activation(out=cact, in_=cT, func=ACT.Silu), reads=["cT"], writes=["cact"])

    def mods_compute(wada, nblk, bada_sb, bada_key, out_sb, out_key):
        mk = ar.mark()
        wm = [ar.alloc([8, 1024], BF16) for _ in range(2)]
        for mi in range(nblk):
            s_ = mi % 2
            p.dma("pool", lambda e, s_=s_, mi=mi: e.dma_start(out=wm[s_], in_=wada[mi].rearrange("q (k c) -> q k c", k=8)),
                  ("wm", s_), writes=[("wm", s_)])
            fns = []
            for ch in range(8):
                for kc in range(8):
                    fns.append(lambda e, s_=s_, ch=ch, kc=kc, mi=mi: e.matmul(
                        ps[7][:, mi * 8 + ch:mi * 8 + ch + 1], wm[s_][:, kc, ch * 128:(ch + 1) * 128],
                        cact[:, kc:kc + 1], start=(kc == 0), stop=(kc == 7)))
            p.mm(fns, reads=[("wm", s_), "cact"], writes=[("ps", 7)])
        p.op("dve", lambda e: e.tensor_tensor(out_sb, ps[7][:, 0:nblk * 8], bada_sb, ALU.add),
             reads=[("ps", 7), bada_key], writes=[out_key])
        p.barrier()
        ar.reset(mk)

    def mods_derive(l):
        m = M[l]
        ng_in = din("ng%d" % l, [128, 24])
        p.dma("sp", lambda e: e.dma_start(out=m["ng"], in_=ng_in), ("c2", l), writes=[("ng", l)])
        for k in range(3):
            p.op("dve", lambda e, k=k: e.scalar_tensor_tensor(
                m["gs"][k], m["mod"][:, (3 * k + 1) * 8:(3 * k + 2) * 8], 1.0, m["ng"][:, k * 8:(k + 1) * 8],
                ALU.add, ALU.mult), reads=[("mod", l), ("ng", l)], writes=[("gs", l, k)])
            p.op("dve", lambda e, k=k: e.tensor_scalar(
                m["gd"][k], m["mod"][:, (3 * k + 2) * 8:(3 * k + 3) * 8], 0.5 if k != 1 else 1.0, None, ALU.mult),
                reads=[("mod", l)], writes=[("gd", l, k)])

    def mods(l):
        m = M[l]
        if first:
            wada = din("wada%d" % l, [9, 128, 8 * 1024])
            bada_in = din("bada%d" % l, [128, 72])
            p.dma("sp", lambda e: e.dma_start(out=m["bada"], in_=bada_in), ("c1", l), writes=[("bada", l)])
            mods_compute(wada, 9, m["bada"], ("bada", l), m["mod"], ("mod", l))
            mod_out = dout("mod%d_out" % l, [128, 72])
            p.dma("sp", lambda e: e.dma_start(out=mod_out, in_=m["mod"]), ("mo", l), reads=[("mod", l)])
            wadax = din("wadax", [7, 128, 8 * 1024])
            badax_in = din("badax", [128, 56])
            modx_out = dout("modx_out", [128, 56])
            p.dma("sp", lambda e: e.dma_start(out=m["bada"][:, 0:56], in_=badax_in), ("c1", l), reads=[("bada", l)], writes=[("bada", l)])
            mk = ar.mark()
            modx = ar.alloc([56], F32)
            mods_compute(wadax, 7, m["bada"][:, 0:56], ("bada", l), modx, "modx")
            p.dma("sp", lambda e: e.dma_start(out=modx_out, in_=modx), "mxo", reads=["modx"])
            p.barrier()
            ar.reset(mk)
        else:
            mod_in = din("mod%d" % l, [128, 72])
            p.dma("sp", lambda e: e.dma_start(out=m["mod"], in_=mod_in), ("c1", l), writes=[("mod", l)])
        mods_derive(l)

    def norm(l, k, hT):
        m = M[l]

        def stats1(t):
            s_ = t % 2
            p.op("act", lambda e: e.activation(out=sq[s_], in_=xT[:, :, tsl(t)], func=ACT.Square),
                 reads=[("xT", c, t) for c in range(8)], writes=[("sq", s_)])
            p.mm([lambda e, c=c: e.matmul(ps[7][:, 0:TT], ones_bf, sq[s_][:, c, :], start=(c == 0), stop=(c == 7))
                  for c in range(8)], reads=[("sq", s_), "ones"], writes=[("ps", 7)])

        def stats2(t):
            s_ = t % 2
            p.op("act", lambda e: e.activation(out=tA[s_], in_=ps[7][:, 0:TT], func=ACT.Ln, scale=1.0 / D_MODEL, bias=EPS),
                 reads=[("ps", 7)], writes=[("tA", s_)])
            p.op("act", lambda e: e.activation(out=rstd[s_], in_=tA[s_], func=ACT.Exp, scale=-0.5),
                 reads=[("tA", s_)], writes=[("rstd", s_)])

        def affine(t):
            s_ = t % 2
            for c in range(8):
                b_ = cnt["tb"] % 4
                cnt["tb"] += 1
                p.op("dve", lambda e, c=c, b_=b_: e.tensor_tensor(tB[b_], xT[:, c, tsl(t)], rstd[s_], ALU.mult),
                     reads=[("xT", c, t), ("rstd", s_)], writes=[("tB", b_)])
                gsc = m["gs"][k][:, c:c + 1]
                shc = m["mod"][:, 3 * k * 8 + c:3 * k * 8 + c + 1]
                if c % 2 == 1:
                    p.op("pool", lambda e, c=c, b_=b_, gsc=gsc, shc=shc: e.tensor_scalar(
                        hT[:, c, tsl(t)], tB[b_], gsc, shc, ALU.mult, ALU.add),
                        reads=[("tB", b_), ("gs", l, k), ("mod", l)], writes=[("hT", t)])
                else:
                    p.op("act", lambda e, c=c, b_=b_, gsc=gsc, shc=shc: e.activation(
                        out=hT[:, c, tsl(t)], in_=tB[b_], func=ACT.Identity, scale=gsc, bias=shc),
                        reads=[("tB", b_), ("gs", l, k), ("mod", l)], writes=[("hT", t)])

        stats1(0)
        stats2(0)
        for t in range(NT):
            if t + 1 < NT:
                stats1(t + 1)
            affine(t)
            if t + 1 < NT:
                stats2(t + 1)

    def proj_mm(slot_ap, skey, src, src_keys, bank, t, nk=8):
        p.mm([lambda e, kc=kc: e.matmul(ps[bank][:, 0:TT], slot_ap[:, kc, :], src[:, kc, tsl(t)],
                                        start=(kc == 0), stop=(kc == nk - 1)) for kc in range(nk)],
             reads=[skey] + src_keys, writes=[("ps", bank)])

    cnt = dict(a=0, b=0, d=0, k=0, stg=0, dbank=0, tb=0, cv=0)

    def load_w(slots, name, w_ap, idx):
        s = cnt[name] % len(slots)
        cnt[name] += 1
        p.dma("pool", lambda e: e.dma_start(out=slots[s], in_=w_ap[idx].rearrange("q (k c) -> q k c", c=128)),
              (name, s), writes=[(name, s)])
        return slots[s], (name, s)

    def ffn(l, j):
        k = 0 if j == 0 else 2
        wg = din("wg%d%d" % (l, j), [NF, 128, 8 * 128])
        wu = din("wu%d%d" % (l, j), [NF, 128, 8 * 128])
        wd = din("wd%d%d" % (l, j), [8, 128, NF * 128])
        mk = ar.mark()
        hT = ar.alloc([8, NTOK], BF16)
        aT = ar.alloc([NF, 3 * TT], BF16)
        gd = M[l]["gd"][k]
        items = []
        for half in range(2):
            items += [("gu", half, f) for f in range(NF)] + [("d", half, d) for d in range(8)]
        loaded, nxt = {}, [0]

        def ensure(upto):
            while nxt[0] <= min(upto, len(items) - 1):
                it = items[nxt[0]]
                if it[0] == "gu":
                    loaded[nxt[0]] = (load_w(wA, "a", wg, it[2]), load_w(wB, "b", wu, it[2]))
                else:
                    loaded[nxt[0]] = (load_w(wD, "d", wd, it[2]),)
                nxt[0] += 1

        ensure(2)
        norm(l, k, hT)
        for i, it in enumerate(items):
            ensure(i + 2)
            tiles = [it[1] * 3 + x for x in range(3)]
            if it[0] == "gu":
                f = it[2]
                (a_ap, a_key), (b_ap, b_key) = loaded.pop(i)
                for ti, t in enumerate(tiles):
                    q = cnt["k"] % 2
                    cnt["k"] += 1
                    proj_mm(a_ap, a_key, hT, [("hT", t)], q, t)
                    proj_mm(b_ap, b_key, hT, [("hT", t)], 2 + q, t)
                    p.op("act", lambda e, q=q: e.activation(out=sg[q], in_=ps[q][:, 0:TT], func=ACT.Silu),
                         reads=[("ps", q)], writes=[("sg", q)])
                    p.op("dve", lambda e, q=q, f=f, ti=ti: e.tensor_tensor(
                        aT[:, f, ti * TT:(ti + 1) * TT], sg[q], ps[2 + q][:, 0:TT], ALU.mult),
                        reads=[("sg", q), ("ps", 2 + q)], writes=[("aT", f, ti)])
            else:
                d = it[2]
                ((d_ap, d_key),) = loaded.pop(i)
                for ti, t in enumerate(tiles):
                    bank = 4 + cnt["dbank"] % 3
                    cnt["dbank"] += 1
                    p.mm([lambda e, f=f, ti=ti, bank=bank, d_ap=d_ap: e.matmul(
                        ps[bank][:, 0:TT], d_ap[:, f, :], aT[:, f, ti * TT:(ti + 1) * TT],
                        start=(f == 0), stop=(f == NF - 1)) for f in range(NF)],
                        reads=[d_key] + [("aT", f, ti) for f in range(NF)], writes=[("ps", bank)])
                    p.op("dve", lambda e, d=d, t=t, bank=bank: e.scalar_tensor_tensor(
                        xT[:, d, tsl(t)], ps[bank][:, 0:TT], gd[:, d:d + 1], xT[:, d, tsl(t)], ALU.mult, ALU.add),
                        reads=[("ps", bank), ("xT", d, t), ("gd", l, k)], writes=[("xT", d, t)])
        p.barrier()
        ar.reset(mk)

    def inproj(l):
        win = din("win%d" % l, [16, 128, 8 * 128])
        qkg_in = din("qkg%d" % l, [128, 2])
        poolw_in = din("poolw%d" % l, [128, 4 * 128])
        pscale_in = din("pscale%d" % l, [128, 4])
        icnt_in = din("icnt", [128, 64])
        q_out = dout("q_out", [512, NTOK], BF16)
        k_out = dout("k_out", [512, NTOK], BF16)
        v_out = dout("v_out", [512, NTOK], BF16)
        p_out = dout("p_out", [512, NTOK], BF16)
        mk = ar.mark()
        hT = ar.alloc([8, NTOK], BF16)
        uT = ar.alloc([4, NTOK], F32)
        qkg = ar.alloc([2], F32)
        poolw32 = ar.alloc([4, 128], F32)
        poolw = ar.alloc([4, 128], BF16)
        pscale = ar.alloc([4], F32)
        icnt = ar.alloc([4, 16], F32)
        p.dma("sp", lambda e: e.dma_start(out=qkg, in_=qkg_in), ("i1", l), writes=["qkg"])
        p.dma("sp", lambda e: e.dma_start(out=poolw32, in_=poolw_in.rearrange("q (g c) -> q g c", g=4)), ("i2", l), writes=["poolw32"])
        p.dma("sp", lambda e: e.dma_start(out=pscale, in_=pscale_in), ("i3", l), writes=["pscale"])
        p.dma("sp", lambda e: e.dma_start(out=icnt, in_=icnt_in.rearrange("q (g c) -> q g c", g=4)), ("i4", l), writes=["icnt"])
        p.op("dve", lambda e: e.tensor_copy(poolw, poolw32), reads=["poolw32"], writes=["poolw"])
        gq8 = ar.alloc([2], F32)
        p.op("dve", lambda e: e.tensor_scalar(gq8[:, 0:1], qkg[:, 0:1], 0.125, None, ALU.mult), reads=["qkg"], writes=["gq8"])
        p.op("dve", lambda e: e.tensor_copy(gq8[:, 1:2], qkg[:, 1:2]), reads=["qkg", "gq8"], writes=["gq8"])
        norm(l, 1, hT)
        work = [(oc, t) for oc in range(8) for t in range(NT)]
        wl = {}

        def P_(i):
            oc, t = work[i]
            if t == 0:
                wl[oc] = load_w(wA, "a", win, oc)
            proj_mm(wl[oc][0], wl[oc][1], hT, [("hT", t)], i % 2, t)

        def Q_(i):
            q = i % 2
            p.op("act", lambda e: e.activation(out=sq[q][:, 0, :], in_=ps[q][:, 0:TT], func=ACT.Square),
                 reads=[("ps", q)], writes=[("sq", q)])

        def B_(i):
            q = i % 2
            p.mm([lambda e: e.matmul(ps[2 + q][:, 0:TT], bd_bf, sq[q][:, 0, :], start=True, stop=True)],
                 reads=[("sq", q), "bd"], writes=[("ps", 2 + q)])

        def L_(i):
            q = i % 2
            p.op("act", lambda e: e.activation(out=tA[q], in_=ps[2 + q][:, 0:TT], func=ACT.Ln, scale=1.0 / 64, bias=EPS),
                 reads=[("ps", 2 + q)], writes=[("tA", q)])
            p.op("act", lambda e: e.activation(out=rstd[q], in_=tA[q], func=ACT.Exp, scale=-0.5),
                 reads=[("tA", q)], writes=[("rstd", q)])

        def D_(i):
            oc, t = work[i]
            q = i % 2
            si = cnt["stg"] % 2
            cnt["stg"] += 1
            sb = stg[si].bitcast(BF16)[:, 0:TT]
            col = 0 if oc < 4 else 1
            p.op("dve", lambda e: e.scalar_tensor_tensor(sb, ps[q][:, 0:TT], gq8[:, col:col + 1], rstd[q], ALU.mult, ALU.mult),
                 reads=[("ps", q), ("rstd", q), "gq8"], writes=[("stg", si)])
            dst = (q_out if oc < 4 else k_out)[(oc % 4) * 128:(oc % 4 + 1) * 128, tsl(t)]
            p.dma("sp", lambda e: e.dma_start(out=dst, in_=sb), ("stg", si), reads=[("stg", si)])

        nw = len(work)
        P_(0)
        Q_(0)
        for i in range(nw):
            if i + 1 < nw:
                P_(i + 1)
            B_(i)
            if i + 1 < nw:
                Q_(i + 1)
            L_(i)
            D_(i)
        for oc in range(8, 16):
            w_ap, w_key = load_w(wA, "a", win, oc)
            for t in range(NT):
                q = cnt["k"] % 2
                cnt["k"] += 1
                proj_mm(w_ap, w_key, hT, [("hT", t)], q, t)
                if oc < 8:
                    pass
                elif oc < 12:
                    si = cnt["stg"] % 2
                    cnt["stg"] += 1
                    sb = stg[si].bitcast(BF16)[:, 0:TT]
                    p.op("act", lambda e, q=q, sb=sb: e.copy(sb, ps[q][:, 0:TT]), reads=[("ps", q)], writes=[("stg", si)])
                    dst = v_out[(oc - 8) * 128:(oc - 7) * 128, tsl(t)]
                    p.dma("sp", lambda e, sb=sb, dst=dst: e.dma_start(out=dst, in_=sb), ("stg", si), reads=[("stg", si)])
                else:
                    p.op("act", lambda e, q=q, oc=oc, t=t: e.copy(uT[:, oc - 12, tsl(t)], ps[q][:, 0:TT]),
                         reads=[("ps", q)], writes=[("uT", oc - 12)])
        p.barrier()
        pa = hT[:, 0:2, :].rearrange("q a t -> q (a t)").bitcast(F32)
        pb = hT[:, 2:4, :].rearrange("q a t -> q (a t)").bitcast(F32)
        dT = hT[:, 4:8, :]
        for g, w in enumerate(POOL_W):
            u = uT[:, g, :]
            p.op("dve", lambda e, u=u: e.tensor_tensor(u[:, 0:TT], u[:, 0:TT], valid, ALU.mult), reads=[("uT", g), "valid"], writes=[("uT", g)])
            src, step, bufs, bi = u, 1, [pa, pb], 0
            while step < w:
                dstb = bufs[bi]
                p.op("dve", lambda e, dstb=dstb, src=src, step=step: e.tensor_tensor(
                    dstb[:, step:NTOK], src[:, step:NTOK], src[:, 0:NTOK - step], ALU.add),
                    reads=[("uT", g), "pa", "pb"], writes=["pa" if bi == 0 else "pb"])
                p.op("dve", lambda e, dstb=dstb, src=src, step=step: e.tensor_copy(dstb[:, 0:step], src[:, 0:step]),
                     reads=[("uT", g), "pa", "pb"], writes=["pa" if bi == 0 else "pb"])
                src, step, bi = dstb, step * 2, 1 - bi
            S = src
            p.op("dve", lambda e, S=S, u=u, g=g, w=w: e.scalar_tensor_tensor(
                dT[:, g, :], S, 1.0 / w, u, ALU.mult, ALU.subtract), reads=["pa", "pb", ("uT", g)], writes=[("dT", g)])
            p.op("dve", lambda e, S=S, g=g: e.tensor_tensor(tA[0][:, 0:16], S[:, HALO:HALO + 16], icnt[:, g, :], ALU.mult),
                 reads=["pa", "pb", "icnt"], writes=["tA"])
            p.op("dve", lambda e, u=u, g=g: e.tensor_tensor(dT[:, g, HALO:HALO + 16], tA[0][:, 0:16], u[:, HALO:HALO + 16], ALU.subtract),
                 reads=["tA", ("uT", g)], writes=[("dT", g)])
            for t in range(NT):
                q = cnt["k"] % 2
                cnt["k"] += 1
                p.mm([lambda e, q=q, g=g, t=t: e.matmul(ps[q][:, 0:TT], poolw[:, g, :], dT[:, g, tsl(t)], start=True, stop=True)],
                     reads=["poolw", ("dT", g)], writes=[("ps", q)])
                si = cnt["stg"] % 2
                cnt["stg"] += 1
                sb = stg[si].bitcast(BF16)[:, 0:TT]
                p.op("act", lambda e, q=q, sb=sb, g=g: e.activation(out=sb, in_=ps[q][:, 0:TT], func=ACT.Copy, scale=pscale[:, g:g + 1]),
                     reads=[("ps", q), "pscale"], writes=[("stg", si)])
                dst = p_out[g * 128:(g + 1) * 128, tsl(t)]
                p.dma("sp", lambda e, sb=sb, dst=dst: e.dma_start(out=dst, in_=sb), ("stg", si), reads=[("stg", si)])
        p.barrier()
        ar.reset(mk)

    def out_proj(l, wname, srcT, src_keys):
        wout = din(wname, [8, 128, 8 * 128])
        g2 = M[l]["gd"][1]
        for d in range(8):
            w_ap, w_key = load_w(wA, "a", wout, d)
            for t in range(NT):
                q = cnt["k"] % 2
                cnt["k"] += 1
                proj_mm(w_ap, w_key, srcT, src_keys(t), q, t)
                p.op("dve", lambda e, q=q, d=d, t=t: e.scalar_tensor_tensor(
                    xT[:, d, tsl(t)], ps[q][:, 0:TT], g2[:, d:d + 1], xT[:, d, tsl(t)], ALU.mult, ALU.add),
                    reads=[("ps", q), ("xT", d, t), ("gd", l, 1)], writes=[("xT", d, t)])

    def mixout(l):
        o_in = din("o_in", [512, NTOK], BF16)
        p_in = din("p_in", [512, NTOK], BF16)
        mk = ar.mark()
        opT = ar.alloc([8, NTOK], BF16)
        p.dma("sp", lambda e: e.dma_start(out=opT[:, 0:4, :], in_=o_in.rearrange("(c q) t -> q c t", q=128)), "oin", writes=["opT0"])
        p.dma("sp", lambda e: e.dma_start(out=opT[:, 4:8, :], in_=p_in.rearrange("(c q) t -> q c t", q=128)), "pin", writes=["opT1"])
        out_proj(l, "wout%d" % l, opT, lambda t: ["opT0", "opT1"])
        p.barrier()
        ar.reset(mk)

    def conv(l):
        win = din("win%d" % l, [24, 128, 8 * 128])
        cw_in = din("convw%d" % l, [128, 24])
        mk = ar.mark()
        hT = ar.alloc([8, NTOK], BF16)
        yT = ar.alloc([8, NTOK], BF16)
        NB = 3
        vb = [ar.alloc([TT + 2], F32) for _ in range(NB)]
        yb = [ar.alloc([TT], F32) for _ in range(NB)]
        tC = [ar.alloc([TT], F32) for _ in range(NB)]
        bS = [ar.alloc([TT], F32) for _ in range(NB)]
        cw = ar.alloc([8, 3], F32)
        p.dma("sp", lambda e: e.dma_start(out=cw, in_=cw_in.rearrange("q (c j) -> q c j", j=3)), ("cw", l), writes=["cw"])
        witems = [(fc, kind) for fc in range(8) for kind in range(3)]
        wl, nxt = {}, [0]

        def ensure(upto):
            while nxt[0] <= min(upto, len(witems) - 1):
                fc, kind = witems[nxt[0]]
                wl[nxt[0]] = load_w(wA if kind != 1 else wB, "a" if kind != 1 else "b", win, kind * 8 + fc)
                nxt[0] += 1

        ensure(2)
        norm(l, 1, hT)
        for fc in range(8):
            ensure(3 * fc + 4)
            (wb_ap, wb_key), (wc_ap, wc_key), (wx_ap, wx_key) = wl.pop(3 * fc), wl.pop(3 * fc + 1), wl.pop(3 * fc + 2)
            for t in range(NT):
                q = cnt["k"] % 2
                cnt["k"] += 1
                r = cnt["cv"] % NB
                rp = (cnt["cv"] - 1) % NB
                cnt["cv"] += 1
                v_, vp_, y_ = vb[r], vb[rp], yb[r]
                proj_mm(wb_ap, wb_key, hT, [("hT", t)], q, t)
                proj_mm(wc_ap, wc_key, hT, [("hT", t)], 2 + q, t)
                proj_mm(wx_ap, wx_key, hT, [("hT", t)], 4 + q, t)
                p.op("act", lambda e, q=q, r=r: e.copy(tC[r], ps[2 + q][:, 0:TT]), reads=[("ps", 2 + q)], writes=[("tC", r)])
                p.op("act", lambda e, q=q, r=r: e.copy(bS[r], ps[q][:, 0:TT]), reads=[("ps", q)], writes=[("bS", r)])
                if t == 0:
                    p.op("pool", lambda e, v_=v_: e.memset(v_[:, 0:2], 0.0), reads=[("vb", r)], writes=[("vb", r)])
                else:
                    p.op("pool", lambda e, v_=v_, vp_=vp_: e.tensor_copy(v_[:, 0:2], vp_[:, TT:TT + 2]),
                         reads=[("vb", rp), ("vb", r)], writes=[("vb", r)])
                p.op("dve", lambda e, q=q, r=r, v_=v_: e.tensor_tensor(v_[:, 2:TT + 2], tC[r], ps[4 + q][:, 0:TT], ALU.mult),
                     reads=[("tC", r), ("ps", 4 + q), ("vb", r)], writes=[("vb", r)])
                if t == 0:
                    p.op("dve", lambda e, v_=v_: e.tensor_tensor(v_[:, 2:TT + 2], v_[:, 2:TT + 2], valid, ALU.mult),
                         reads=[("vb", r), "valid"], writes=[("vb", r)])
                p.op("act", lambda e, fc=fc, v_=v_, y_=y_: e.activation(out=y_, in_=v_[:, 2:TT + 2], func=ACT.Identity, scale=cw[:, fc, 2:3]),
                     reads=[("vb", r), "cw", ("yb", r)], writes=[("yb", r)])
                p.op("dve", lambda e, fc=fc, v_=v_, y_=y_: e.scalar_tensor_tensor(y_, v_[:, 1:TT + 1], cw[:, fc, 1:2], y_, ALU.mult, ALU.add),
                     reads=[("vb", r), "cw", ("yb", r)], writes=[("yb", r)])
                p.op("dve", lambda e, fc=fc, v_=v_, y_=y_: e.scalar_tensor_tensor(y_, v_[:, 0:TT], cw[:, fc, 0:1], y_, ALU.mult, ALU.add),
                     reads=[("vb", r), "cw", ("yb", r)], writes=[("yb", r)])
                p.op("dve", lambda e, fc=fc, t=t, r=r, y_=y_: e.tensor_tensor(yT[:, fc, tsl(t)], y_, bS[r], ALU.mult),
                     reads=[("yb", r), ("bS", r)], writes=[("yT", fc)])
        out_proj(l, "wout%d" % l, yT, lambda t: [("yT", fc) for fc in range(8)])
        p.barrier()
        ar.reset(mk)

    for l in layers:
        mods(l)
    for st in stages:
        if st[0] == "ffn":
            ffn(st[1], st[2])
        elif st[0] == "inproj":
            inproj(st[1])
        elif st[0] == "mixout":
            mixout(st[1])
        elif st[0] == "conv":
            conv(st[1])
    x_out = dout("x_out", [D_MODEL, NTOK])
    p.dma("sp", lambda e: e.dma_start(out=x_out.rearrange("(c q) t -> q c t", q=128), in_=xT), "xo",
          reads=[("xT", c, t) for c in range(8) for t in range(NT)])
    p.emit()
    es.close()
    return nc, names_in, names_out


def _tile_w(W, ncol=128):
    K, N = W.shape
    return np.ascontiguousarray(
        W.reshape(K // 128, 128, N // ncol, ncol).transpose(2, 1, 0, 3).reshape(N // ncol, 128, (K // 128) * ncol))


class host_prep:
    def __init__(self, inp):
        self.inp = inp
        self.cache = {}

    def w(self, name):
        if name in self.cache:
            return self.cache[name]
        I = self.inp
        if name.startswith("wada"):
            v = _tile_w(np.asarray(I["w_ada"][int(name[4:])]), 1024)
        elif name.startswith("bada"):
            v = np.ascontiguousarray(np.asarray(I["b_ada"][int(name[4:])]).reshape(72, 128).T)
        elif name.startswith("ng"):
            v = np.ascontiguousarray(np.asarray(I["norm_g"][int(name[2:])]).reshape(24, 128).T)
        elif name[:2] in ("wg", "wu", "wd"):
            l, j = int(name[2]), int(name[3])
            v = _tile_w(np.asarray(I["ffn_" + name[:2]][l, j]))
        elif name.startswith("win"):
            l = int(name[3:])
            v = _tile_w(np.asarray(I["w_in_ab"][l // 2] if l % 2 == 0 else I["w_in_c"][l // 2]))
        elif name.startswith("wout"):
            l = int(name[4:])
            v = _tile_w(np.asarray(I["w_out_ab"][l // 2] if l % 2 == 0 else I["w_out_c"][l // 2]))
        elif name.startswith("qkg"):
            g = np.asarray(I["qk_norm_g"][int(name[3:]) // 2])
            v = np.ascontiguousarray(np.stack([np.tile(g[0], 2), np.tile(g[1], 2)], axis=1))
        elif name.startswith("poolw"):
            pw = np.asarray(I["pool_w"][int(name[5:]) // 2])
            v = np.ascontiguousarray(pw.transpose(1, 0, 2).reshape(128, 4 * 128))
        elif name.startswith("pscale"):
            v = np.ascontiguousarray(np.asarray(I["pool_scale"][int(name[6:]) // 2]).reshape(4, 128).T)
        elif name.startswith("convw"):
            cw = np.asarray(I["conv_w"][int(name[5:]) // 2])
            v = np.ascontiguousarray(cw.reshape(3, 8, 128).transpose(2, 1, 0).reshape(128, 24))
        else:
            raise KeyError(name)
        self.cache[name] = v
        return v

    def t_inputs(self, core, names, x_full=None, carry=None):
        b, j = core // 4, core % 4
        m = {}
        for n in names:
            if n == "x_in":
                if carry is not None:
                    m[n] = carry[core]["x_out"]
                else:
                    xs = np.zeros((NTOK, D_MODEL), np.float32)
                    lo = j * TOK - HALO
                    src = np.asarray(x_full[b, max(lo, 0):(j + 1) * TOK])
                    xs[NTOK - src.shape[0]:] = src
                    m[n] = np.ascontiguousarray(xs.T)
            elif n == "valid":
                v = np.ones((128, TT), np.float32)
                if j == 0:
                    v[:, :HALO] = 0.0
                m[n] = v
            elif n == "icnt":
                ic = np.zeros((128, 4, 16), np.float32)
                for g, w in enumerate(POOL_W):
                    for t in range(16):
                        ic[:, g, t] = np.float32(1.0) / np.float32(min(t + 1, w) if j == 0 else w)
                m[n] = ic.reshape(128, 64)
            elif n == "cT":
                m[n] = np.ascontiguousarray(np.asarray(self.inp["c"][b]).reshape(8, 128).T)
            elif n in ("o_in", "p_in") or (n.startswith("mod") and n[3:].isdigit()):
                m[n] = carry[core][n]
            elif n == "wadax":
                blk = [i for i in range(27) if i % 4 == j]
                w = np.zeros((7, 128, 8 * 1024), np.float32)
                for pos, i in enumerate(blk):
                    w[pos] = self.w("wada%d" % (1 + i // 9))[i % 9]
                m[n] = w
            elif n == "badax":
                blk = [i for i in range(27) if i % 4 == j]
                bb = np.zeros((128, 56), np.float32)
                for pos, i in enumerate(blk):
                    bb[:, pos * 8:(pos + 1) * 8] = self.w("bada%d" % (1 + i // 9))[:, (i % 9) * 8:(i % 9 + 1) * 8]
                m[n] = bb
            else:
                m[n] = self.w(n)
        return m


QT = 512
NQT = SEQ // QT
NKB = SEQ // 128


def build_B(l):
    nc = bass.Bass("TRN2", target_bir_lowering=False)
    es = contextlib.ExitStack()
    p = Prog(nc, es)
    ar = Arena(nc, es, 30000)
    ps = [es.enter_context(nc.psum_tensor("ps%d" % i, [128, 512], F32)) for i in range(8)]
    q_in = nc.dram_tensor("qT", [128, SEQ], BF16, kind="ExternalInput").ap()
    k_in = nc.dram_tensor("kT", [128, SEQ], BF16, kind="ExternalInput").ap()
    v_in = nc.dram_tensor("v", [128, NKB * 128], BF16, kind="ExternalInput").ap()
    lam_in = nc.dram_tensor("lam", [128, 256], F32, kind="ExternalInput").ap()
    sg_in = nc.dram_tensor("sg", [128, 1], F32, kind="ExternalInput").ap()
    tri_in = nc.dram_tensor("tri", [128, 128], F32, kind="ExternalInput").ap()
    o_out = nc.dram_tensor("oT", [128, SEQ], BF16, kind="ExternalOutput").ap()

    qT = ar.alloc([SEQ], BF16)
    kT = ar.alloc([SEQ], BF16)
    V = ar.alloc([NKB, 128], BF16)
    ones_bf = ar.alloc([128], BF16)
    tri32 = ar.alloc([128], F32)
    tri = ar.alloc([128], BF16)
    P1 = [ar.alloc([QT], BF16) for _ in range(2)]
    P2 = [ar.alloc([QT], BF16) for _ in range(2)]
    LA = [[ar.alloc([QT], F32) for _ in range(2)] for _ in range(2)]
    Lh = [ar.alloc([QT], BF16) for _ in range(2)]
    Ll = [ar.alloc([QT], BF16) for _ in range(2)]
    lam = ar.alloc([256], F32)
    prod = ar.alloc([128], F32)
    sc = ar.alloc([8], F32)
    sgv = ar.alloc([1], F32)
    lnv = ar.alloc([QT], F32)
    rl = [ar.alloc([QT], F32) for _ in range(2)]
    t1 = [ar.alloc([QT], F32) for _ in range(2)]
    t2 = ar.alloc([QT], F32)
    rs = ar.alloc([QT], F32)
    sqo = ar.alloc([QT], BF16)
    ob = [ar.alloc([QT], BF16) for _ in range(2)]

    for i in range(4):
        sl = slice(i * 2048, (i + 1) * 2048)
        p.dma("sp", lambda e, sl=sl: e.dma_start(out=qT[:, sl], in_=q_in[:, sl]), ("q", i), writes=[("q", i)])
        p.dma("sp", lambda e, sl=sl: e.dma_start(out=kT[:, sl], in_=k_in[:, sl]), ("k", i), writes=[("k", i)])
        p.dma("sp", lambda e, i=i: e.dma_start(out=V[:, i * 16:(i + 1) * 16, :],
                                               in_=v_in[:, i * 2048:(i + 1) * 2048].rearrange("q (b d) -> q b d", d=128)),
              ("v", i), writes=[("v", i)])
    p.dma("sp", lambda e: e.dma_start(out=lam, in_=lam_in), "c1", writes=["lam"])
    p.dma("sp", lambda e: e.dma_start(out=sgv, in_=sg_in), "c2", writes=["sgv"])
    p.dma("sp", lambda e: e.dma_start(out=tri32, in_=tri_in), "c3", writes=["tri32"])
    p.op("dve", lambda e: e.memset(ones_bf, 1.0), writes=["ones"])
    p.op("dve", lambda e: e.tensor_copy(tri, tri32), reads=["tri32"], writes=["tri"])
    li = LAM_INIT[l]
    for j in range(2):
        p.op("dve", lambda e, j=j: e.tensor_tensor(prod[:, 0:64], lam[:, j * 128:j * 128 + 64], lam[:, j * 128 + 64:j * 128 + 128], ALU.mult),
             reads=["lam"], writes=["prod"])
        p.op("dve", lambda e, j=j: e.reduce_sum(sc[:, j:j + 1], prod[:, 0:64], axis=mybir.AxisListType.X),
             reads=["prod"], writes=[("sc", j)])
        p.op("act", lambda e, j=j: e.activation(out=sc[:, 2 + j:3 + j], in_=sc[:, j:j + 1], func=ACT.Exp),
             reads=[("sc", j)], writes=[("sc", 2 + j)])
    p.op("dve", lambda e: e.tensor_tensor(sc[:, 4:5], sc[:, 3:4], sc[:, 2:3], ALU.subtract),
         reads=[("sc", 2), ("sc", 3)], writes=[("sc", 4)])
    p.op("dve", lambda e: e.tensor_scalar(sc[:, 5:6], sc[:, 4:5], -li, None, ALU.add), reads=[("sc", 4)], writes=["nlam"])
    p.op("dve", lambda e: e.tensor_scalar(sc[:, 6:7], sgv, 1.0 - li, None, ALU.mult), reads=["sgv"], writes=["sgs"])
    nlam = sc[:, 5:6]
    sgs = sc[:, 6:7]

    steps = [(qt, kb) for qt in range(NQT) for kb in range(4 * qt + 4)]

    def geom(n):
        qt, kb = steps[n]
        i = kb - 4 * qt
        return qt, kb, i, max(0, i) * 128, n % 2

    def qk(n):
        qt, kb, i, c0, b = geom(n)
        q0 = qt * QT
        kcols = slice(kb * 128, (kb + 1) * 128)
        rk = [("k", kb // 16), ("q", qt // 4)]
        p.mm([lambda e: e.matmul(ps[b][:, c0:QT], kT[0:64, kcols], qT[0:64, q0 + c0:q0 + QT], start=True, stop=True),
              lambda e: e.matmul(ps[2 + b][:, c0:QT], kT[64:128, kcols], qT[64:128, q0 + c0:q0 + QT], start=True, stop=True)],
             reads=rk, writes=[("ps", b), ("ps", 2 + b)])

    def softmax_pv(n):
        qt, kb, i, c0, b = geom(n)
        par = qt % 2
        nkb = 4 * qt + 4
        p.op("act", lambda e: e.activation(out=P1[b][:, c0:QT], in_=ps[b][:, c0:QT], func=ACT.Exp),
             reads=[("ps", b)], writes=[("P1", b)])
        p.op("act", lambda e: e.activation(out=P2[b][:, c0:QT], in_=ps[2 + b][:, c0:QT], func=ACT.Exp),
             reads=[("ps", 2 + b)], writes=[("P2", b)])
        if i >= 0:
            p.op("dve", lambda e: e.tensor_tensor(P1[b][:, c0:c0 + 128], P1[b][:, c0:c0 + 128], tri, ALU.mult),
                 reads=[("P1", b), "tri"], writes=[("P1", b)])
            p.op("pool", lambda e: e.tensor_tensor(P2[b][:, c0:c0 + 128], P2[b][:, c0:c0 + 128], tri, ALU.mult),
                 reads=[("P2", b), "tri"], writes=[("P2", b)])
        st, sp_ = (kb == 0), (kb == nkb - 1)
        p.mm([lambda e: e.matmul(ps[4 + par][:, c0:QT], V[:, kb, :], P1[b][:, c0:QT], start=st, stop=sp_),
              lambda e: e.matmul(ps[6 + par][:, c0:QT], V[:, kb, :], P2[b][:, c0:QT], start=st, stop=sp_)],
             reads=[("P1", b), ("P2", b), ("v", kb // 16)], writes=[("ps", 4 + par), ("ps", 6 + par)])
        for m_, Pm in ((0, P1), (1, P2)):
            pk = ("P%d" % (m_ + 1), b)
            for eng, lo, hi, part in (("dve", c0, QT, 0),):
                if lo >= hi:
                    continue
                if kb == 0:
                    p.op(eng, lambda e, m_=m_, Pm=Pm, lo=lo, hi=hi: e.tensor_copy(LA[m_][par][:, lo:hi], Pm[b][:, lo:hi]),
                         reads=[pk], writes=[("LA", m_, par, part)])
                else:
                    p.op(eng, lambda e, m_=m_, Pm=Pm, lo=lo, hi=hi: e.tensor_tensor(
                        LA[m_][par][:, lo:hi], LA[m_][par][:, lo:hi], Pm[b][:, lo:hi], ALU.add),
                        reads=[pk, ("LA", m_, par, part)], writes=[("LA", m_, par, part)])

    def fin1(qt, b):
        par = qt % 2
        for m_ in range(2):
            p.op("dve", lambda e, m_=m_: e.tensor_copy(Lh[m_], LA[m_][par]),
                 reads=[("LA", m_, par, 0)], writes=[("Lh", m_)])
            p.op("dve", lambda e, m_=m_: e.tensor_tensor(Ll[m_], LA[m_][par], Lh[m_], ALU.subtract),
                 reads=[("LA", m_, par, 0), ("Lh", m_)], writes=[("Ll", m_)])
            bank = b + 2 * m_
            p.mm([lambda e, m_=m_, bank=bank: e.matmul(ps[bank][:, 0:QT], ones_bf, Lh[m_], start=True, stop=False),
                  lambda e, m_=m_, bank=bank: e.matmul(ps[bank][:, 0:QT], ones_bf, Ll[m_], start=False, stop=True)],
                 reads=[("Lh", m_), ("Ll", m_), "ones"], writes=[("ps", bank)])
            p.op("act", lambda e, bank=bank: e.activation(out=lnv, in_=ps[bank][:, 0:QT], func=ACT.Ln),
                 reads=[("ps", bank)], writes=["lnv"])
            p.op("act", lambda e, m_=m_: e.activation(out=rl[m_], in_=lnv, func=ACT.Exp, scale=-1.0),
                 reads=["lnv"], writes=[("rl", m_)])
        p.op("dve", lambda e: e.tensor_tensor(t1[par], ps[4 + par][:, 0:QT], rl[0], ALU.mult),
             reads=[("ps", 4 + par), ("rl", 0)], writes=[("t1", par)])
        p.op("dve", lambda e: e.tensor_tensor(t2, ps[6 + par][:, 0:QT], rl[1], ALU.mult),
             reads=[("ps", 6 + par), ("rl", 1)], writes=["t2"])
        p.op("dve", lambda e: e.scalar_tensor_tensor(t1[par], t2, nlam, t1[par], ALU.mult, ALU.add),
             reads=["t2", ("t1", par), "nlam"], writes=[("t1", par)])
        p.op("act", lambda e: e.activation(out=sqo, in_=t1[par], func=ACT.Square), reads=[("t1", par)], writes=["sqo"])
        p.mm([lambda e: e.matmul(ps[4 + par][:, 0:QT], ones_bf, sqo, start=True, stop=True)],
             reads=["sqo", "ones"], writes=[("ps", 4 + par)])

    def fin2(qt):
        par = qt % 2
        q0 = qt * QT
        p.op("act", lambda e: e.activation(out=lnv, in_=ps[4 + par][:, 0:QT], func=ACT.Ln, scale=1.0 / 128, bias=EPS),
             reads=[("ps", 4 + par)], writes=["lnv"])
        p.op("act", lambda e: e.activation(out=rs, in_=lnv, func=ACT.Exp, scale=-0.5), reads=["lnv"], writes=["rs"])
        o_b = ob[par]
        p.op("dve", lambda e: e.scalar_tensor_tensor(o_b, t1[par], sgs, rs, ALU.mult, ALU.mult),
             reads=[("t1", par), "sgs", "rs"], writes=[("ob", par)])
        p.dma("sp", lambda e: e.dma_start(out=o_out[:, q0:q0 + QT], in_=o_b), ("ob", par), reads=[("ob", par)])

    N = len(steps)
    pend1, pend2 = [], []
    qk(0)
    for n in range(N):
        if n + 1 < N:
            qk(n + 1)
        softmax_pv(n)
        for (due, qt_) in list(pend2):
            if due <= n:
                fin2(qt_)
                pend2.remove((due, qt_))
        for (due, qt_) in list(pend1):
            if due <= n:
                fin1(qt_, n % 2)
                pend1.remove((due, qt_))
                pend2.append((n + 1, qt_))
        qt, kb = steps[n]
        if kb == 4 * qt + 3:
            pend1.append((n + 1, qt))
    for (due, qt_) in pend2:
        fin2(qt_)
    for (due, qt_) in pend1:
        fin1(qt_, N % 2)
        fin2(qt_)
    p.emit()
    es.close()
    return nc


def _b_inputs(H, rT, l, core):
    b, h = core // 4, core % 4
    I = H.inp
    e = l // 2
    rows = slice(h * 128, (h + 1) * 128)
    qT = np.concatenate([rT[b * 4 + j]["q_out"][rows, HALO:] for j in range(4)], axis=1)
    kT = np.concatenate([rT[b * 4 + j]["k_out"][rows, HALO:] for j in range(4)], axis=1)
    vT = np.concatenate([rT[b * 4 + j]["v_out"][rows, HALO:] for j in range(4)], axis=1)
    v = np.ascontiguousarray(vT.T.reshape(NKB, 128, 128).transpose(1, 0, 2).reshape(128, NKB * 128))
    lam = np.concatenate([np.asarray(I[n][e]) for n in ("lambda_q1", "lambda_k1", "lambda_q2", "lambda_k2")])
    lam = np.ascontiguousarray(np.broadcast_to(lam[None, :], (128, 256))).astype(np.float32)
    sg = np.ascontiguousarray(np.asarray(I["subln_g"][e]).reshape(128, 1))
    tri = (np.arange(128)[:, None] <= np.arange(128)[None, :]).astype(np.float32)
    return {"qT": np.ascontiguousarray(qT), "kT": np.ascontiguousarray(kT), "v": v, "lam": lam, "sg": sg, "tri": tri}


def _o_for(rB, core):
    b, j = core // 4, core % 4
    lo = j * TOK - HALO
    o = np.zeros((512, NTOK), dtype=rB[0]["oT"].dtype)
    for h in range(4):
        src = rB[b * 4 + h]["oT"][:, max(lo, 0):(j + 1) * TOK]
        o[h * 128:(h + 1) * 128, NTOK - src.shape[1]:] = src
    return o


def _run(nc, in_maps):
    return run_bass_kernel_spmd(nc, in_maps, core_ids=list(range(N_CORES))).results


def _assemble_mods(rA):
    mods = {}
    for b in range(BATCH):
        tabs = {l: np.zeros((128, 72), np.float32) for l in (1, 2, 3)}
        for i in range(27):
            j, pos = i % 4, i // 4
            tabs[1 + i // 9][:, (i % 9) * 8:(i % 9 + 1) * 8] = rA[b * 4 + j]["modx_out"][:, pos * 8:(pos + 1) * 8]
        for j in range(4):
            mods[b * 4 + j] = dict(mod0=rA[b * 4 + j]["mod0_out"], mod1=tabs[1], mod2=tabs[2], mod3=tabs[3])
    return mods


def kernel(**inputs):
    H = host_prep(inputs)
    cores = range(N_CORES)
    nc, nin, _ = build_T([("ffn", 0, 0), ("inproj", 0)], True)
    rA = _run(nc, [H.t_inputs(c, nin, x_full=inputs["x"]) for c in cores])
    mods = _assemble_mods(rA)
    rB = _run(build_B(0), [_b_inputs(H, rA, 0, c) for c in cores])
    carry = [dict(x_out=rA[c]["x_out"], o_in=_o_for(rB, c), p_in=rA[c]["p_out"], **mods[c]) for c in cores]
    nc, nin, _ = build_T([("mixout", 0), ("ffn", 0, 1), ("ffn", 1, 0), ("conv", 1), ("ffn", 1, 1),
                          ("ffn", 2, 0), ("inproj", 2)], False)
    rC = _run(nc, [H.t_inputs(c, nin, carry=carry) for c in cores])
    del rA, carry
    rB = _run(build_B(2), [_b_inputs(H, rC, 2, c) for c in cores])
    carry = [dict(x_out=rC[c]["x_out"], o_in=_o_for(rB, c), p_in=rC[c]["p_out"], **mods[c]) for c in cores]
    nc, nin, _ = build_T([("mixout", 2), ("ffn", 2, 1), ("ffn", 3, 0), ("conv", 3), ("ffn", 3, 1)], False)
    rD = _run(nc, [H.t_inputs(c, nin, carry=carry) for c in cores])
    out = np.empty((BATCH, SEQ, D_MODEL), np.float32)
    for c in cores:
        b, j = c // 4, c % 4
        out[b, j * TOK:(j + 1) * TOK] = rD[c]["x_out"][:, HALO:].T
    return out
```

```python
import numpy as np
import concourse.bass as bass
import concourse.mybir as mybir
from concourse.bass_utils import run_bass_kernel_spmd

F32 = mybir.dt.float32
BF16 = mybir.dt.bfloat16
ACT = mybir.ActivationFunctionType
ALU = mybir.AluOpType


import contextlib
import ml_dtypes

N_CORES = 8
D_MODEL, BATCH, SEQ, DEPTH = 1024, 2, 8192, 4
D_FF = 2816
NF = D_FF // 128
TOK = SEQ * BATCH // N_CORES
HALO = 64
NTOK = TOK + HALO
TT = 352
NT = NTOK // TT
EPS = 1e-6
LAM_INIT = {0: 0.8 - 0.6 * float(np.exp(-0.3 * 0)), 2: 0.8 - 0.6 * float(np.exp(-0.3 * 2))}
POOL_W = (2, 4, 8, 16)


class Prog:
    def __init__(self, nc, es):
        self.nc, self.es = nc, es
        self.eng = dict(pe=nc.tensor, act=nc.scalar, dve=nc.vector, pool=nc.gpsimd, sp=nc.sync)
        self.rec = {k: [] for k in self.eng}
        self.sem = {k: es.enter_context(nc.semaphore("s_" + k)) for k in ("pe", "act", "dve", "pool")}
        self.cnt = {k: 0 for k in self.sem}
        self.dsem = {}
        self.waited = {k: {} for k in self.eng}
        self.lastw, self.rd = {}, {}

    def _deps(self, e, reads, writes, nowaw=False):
        toks = []
        for r in reads:
            t = self.lastw.get(r)
            if t:
                toks.append(t)
        for w in writes:
            t = self.lastw.get(w)
            if t and not nowaw:
                toks.append(t)
            toks.extend(self.rd.get(w, {}).values())
        waits = {}
        for (s, v, te) in toks:
            if te == "pe" and e == "pe":
                continue
            k = id(s)
            if self.waited[e].get(k, 0) >= v:
                continue
            if k not in waits or waits[k][1] < v:
                waits[k] = (s, v)
        for k, (s, v) in waits.items():
            self.waited[e][k] = v
        return list(waits.values())

    def _commit(self, tok, reads, writes):
        for w in writes:
            self.lastw[w] = tok
            self.rd[w] = {}
        for r in reads:
            d = self.rd.setdefault(r, {})
            k = id(tok[0])
            if k not in d or d[k][1] < tok[1]:
                d[k] = tok

    def op(self, e, fn, reads=(), writes=()):
        waits = self._deps(e, reads, writes)
        self.cnt[e] += 1
        tok = (self.sem[e], self.cnt[e], e)
        self.rec[e].append((waits, fn, (self.sem[e], 1)))
        self._commit(tok, reads, writes)

    def mm(self, fns, reads=(), writes=()):
        waits = self._deps("pe", reads, writes)
        self.cnt["pe"] += 1
        tok = (self.sem["pe"], self.cnt["pe"], "pe")
        n = len(fns)
        for i, fn in enumerate(fns):
            self.rec["pe"].append((waits if i == 0 else [], fn, (self.sem["pe"], 1) if i == n - 1 else None))
        self._commit(tok, reads, writes)

    def dma(self, q, fn, key, reads=(), writes=(), nowaw=False):
        if key not in self.dsem:
            self.dsem[key] = [self.es.enter_context(self.nc.semaphore("d%d" % len(self.dsem))), 0]
        waits = self._deps(q, reads, writes, nowaw)
        d = self.dsem[key]
        d[1] += 16
        tok = (d[0], d[1], None)
        self.rec[q].append((waits, fn, (d[0], 16)))
        self._commit(tok, reads, writes)

    def barrier(self):
        for e in self.eng:
            waits = []
            for k, s in self.sem.items():
                if k != e and self.cnt[k] > self.waited[e].get(id(s), 0):
                    waits.append((s, self.cnt[k]))
                    self.waited[e][id(s)] = self.cnt[k]
                if k == e and e != "pe" and self.cnt[k] > self.waited[e].get(id(s), 0):
                    waits.append((s, self.cnt[k]))
                    self.waited[e][id(s)] = self.cnt[k]
            for d in self.dsem.values():
                if d[1] > self.waited[e].get(id(d[0]), 0):
                    waits.append((d[0], d[1]))
                    self.waited[e][id(d[0])] = d[1]
            if waits:
                self.rec[e].append((waits, None, None))
        self.lastw, self.rd = {}, {}

    def emit(self):
        self.barrier()
        with self.nc.Block() as block:
            for name, dec in (("pe", block.tensor), ("act", block.scalar), ("dve", block.vector),
                              ("pool", block.gpsimd), ("sp", block.sync)):
                recs = self.rec[name]

                def body(eng, recs=recs):
                    for waits, fn, inc in recs:
                        for s, v in waits:
                            eng.wait_ge(s, v)
                        if fn is None:
                            continue
                        ins = fn(eng)
                        if inc is not None:
                            ins.then_inc(inc[0], inc[1])
                dec(body)


class Arena:
    def __init__(self, nc, es, words):
        self.t = es.enter_context(nc.sbuf_tensor("arena", [128, words], F32))
        self.words, self.off = words, 0

    def mark(self):
        return self.off

    def reset(self, m):
        self.off = m

    def alloc(self, shape_free, dtype):
        n = int(np.prod(shape_free))
        w = n if dtype == F32 else (n + 1) // 2
        w = (w + 7) // 8 * 8
        assert self.off + w <= self.words, ("SBUF arena overflow", self.off, w, self.words)
        ap = self.t[:, self.off:self.off + w]
        self.off += w
        if dtype != F32:
            ap = ap.bitcast(dtype)
        ap = ap[:, 0:n]
        if len(shape_free) == 2:
            ap = ap.rearrange("p (a b) -> p a b", a=shape_free[0])
        elif len(shape_free) == 3:
            ap = ap.rearrange("p (a b c) -> p a b c", a=shape_free[0], b=shape_free[1])
        return ap


def tsl(t):
    return slice(t * TT, (t + 1) * TT)


def build_T(stages, first):
    nc = bass.Bass("TRN2", target_bir_lowering=False)
    es = contextlib.ExitStack()
    p = Prog(nc, es)
    ar = Arena(nc, es, 53000)
    ps = [es.enter_context(nc.psum_tensor("ps%d" % i, [128, 512], F32)) for i in range(8)]
    names_in, names_out = [], []

    def din(name, shape, dt=F32):
        names_in.append(name)
        return nc.dram_tensor(name, list(shape), dt, kind="ExternalInput").ap()

    def dout(name, shape, dt=F32):
        names_out.append(name)
        return nc.dram_tensor(name, list(shape), dt, kind="ExternalOutput").ap()

    layers = sorted({s[1] for s in stages})
    xT = ar.alloc([8, NTOK], F32)
    ones_bf = ar.alloc([128], BF16)
    bd_bf = ar.alloc([128], BF16)
    valid = ar.alloc([TT], F32)
    wA = [ar.alloc([8, 128], BF16) for _ in range(3)]
    wB = [ar.alloc([8, 128], BF16) for _ in range(3)]
    wD = [ar.alloc([NF, 128], BF16) for _ in range(3)]
    sq = [ar.alloc([8, TT], BF16) for _ in range(2)]
    tA = [ar.alloc([TT], F32) for _ in range(2)]
    rstd = [ar.alloc([TT], F32) for _ in range(2)]
    tB = [ar.alloc([TT], F32) for _ in range(4)]
    sg = [ar.alloc([TT], F32) for _ in range(2)]
    stg = sg
    cT = ar.alloc([8], F32)
    cact = ar.alloc([8], BF16)
    M = {}
    for l in layers:
        M[l] = dict(mod=ar.alloc([72], F32), bada=ar.alloc([72], F32), ng=ar.alloc([24], F32),
                    gs=[ar.alloc([8], F32) for _ in range(3)], gd=[ar.alloc([8], F32) for _ in range(3)])
    phase_mark = ar.mark()

    x_in = din("x_in", [D_MODEL, NTOK])
    valid_in = din("valid", [128, TT])
    c_in = din("cT", [128, 8])

    p.op("dve", lambda e: e.memset(ones_bf, 1.0), writes=["ones"])
    p.op("dve", lambda e: e.memset(bd_bf, 0.0), writes=["bd"])
    p.op("dve", lambda e: e.memset(bd_bf[0:64, 0:64], 1.0), writes=["bd"])
    p.op("dve", lambda e: e.memset(bd_bf[64:128, 64:128], 1.0), writes=["bd"])
    p.dma("sp", lambda e: e.dma_start(out=xT, in_=x_in.rearrange("(c q) t -> q c t", q=128)), "x",
          writes=[("xT", c, t) for c in range(8) for t in range(NT)])
    p.dma("sp", lambda e: e.dma_start(out=valid, in_=valid_in), "cst", writes=["valid"])
    p.dma("sp", lambda e: e.dma_start(out=cT, in_=c_in), "cst2", writes=["cT"])
    p.op("act", lambda e: e.activation(out=cact, in_=cT, func=ACT.Silu), reads=["cT"], writes=["cact"])

    def mods_compute(wada, nblk, bada_sb, bada_key, out_sb, out_key):
        mk = ar.mark()
        wm = [ar.alloc([8, 1024], BF16) for _ in range(2)]
        for mi in range(nblk):
            s_ = mi % 2
            p.dma("pool", lambda e, s_=s_, mi=mi: e.dma_start(out=wm[s_], in_=wada[mi].rearrange("q (k c) -> q k c", k=8)),
                  ("wm", s_), writes=[("wm", s_)])
            fns = []
            for ch in range(8):
                for kc in range(8):
                    fns.append(lambda e, s_=s_, ch=ch, kc=kc, mi=mi: e.matmul(
                        ps[7][:, mi * 8 + ch:mi * 8 + ch + 1], wm[s_][:, kc, ch * 128:(ch + 1) * 128],
                        cact[:, kc:kc + 1], start=(kc == 0), stop=(kc == 7)))
            p.mm(fns, reads=[("wm", s_), "cact"], writes=[("ps", 7)])
        p.op("dve", lambda e: e.tensor_tensor(out_sb, ps[7][:, 0:nblk * 8], bada_sb, ALU.add),
             reads=[("ps", 7), bada_key], writes=[out_key])
        p.barrier()
        ar.reset(mk)

    def mods_derive(l):
        m = M[l]
        ng_in = din("ng%d" % l, [128, 24])
        p.dma("sp", lambda e: e.dma_start(out=m["ng"], in_=ng_in), ("c2", l), writes=[("ng", l)])
        for k in range(3):
            p.op("dve", lambda e, k=k: e.scalar_tensor_tensor(
                m["gs"][k], m["mod"][:, (3 * k + 1) * 8:(3 * k + 2) * 8], 1.0, m["ng"][:, k * 8:(k + 1) * 8],
                ALU.add, ALU.mult), reads=[("mod", l), ("ng", l)], writes=[("gs", l, k)])
            p.op("dve", lambda e, k=k: e.tensor_scalar(
                m["gd"][k], m["mod"][:, (3 * k + 2) * 8:(3 * k + 3) * 8], 0.5 if k != 1 else 1.0, None, ALU.mult),
                reads=[("mod", l)], writes=[("gd", l, k)])

    def mods(l):
        m = M[l]
        if first:
            wada = din("wada%d" % l, [9, 128, 8 * 1024])
            bada_in = din("bada%d" % l, [128, 72])
            p.dma("sp", lambda e: e.dma_start(out=m["bada"], in_=bada_in), ("c1", l), writes=[("bada", l)])
            mods_compute(wada, 9, m["bada"], ("bada", l), m["mod"], ("mod", l))
            mod_out = dout("mod%d_out" % l, [128, 72])
            p.dma("sp", lambda e: e.dma_start(out=mod_out, in_=m["mod"]), ("mo", l), reads=[("mod", l)])
            wadax = din("wadax", [7, 128, 8 * 1024])
            badax_in = din("badax", [128, 56])
            modx_out = dout("modx_out", [128, 56])
            p.dma("sp", lambda e: e.dma_start(out=m["bada"][:, 0:56], in_=badax_in), ("c1", l), reads=[("bada", l)], writes=[("bada", l)])
            mk = ar.mark()
            modx = ar.alloc([56], F32)
            mods_compute(wadax, 7, m["bada"][:, 0:56], ("bada", l), modx, "modx")
            p.dma("sp", lambda e: e.dma_start(out=modx_out, in_=modx), "mxo", reads=["modx"])
            p.barrier()
            ar.reset(mk)
        else:
            mod_in = din("mod%d" % l, [128, 72])
            p.dma("sp", lambda e: e.dma_start(out=m["mod"], in_=mod_in), ("c1", l), writes=[("mod", l)])
        mods_derive(l)

    def norm(l, k, hT):
        m = M[l]

        def stats1(t):
            s_ = t % 2
            p.op("act", lambda e: e.activation(out=sq[s_], in_=xT[:, :, tsl(t)], func=ACT.Square),
                 reads=[("xT", c, t) for c in range(8)], writes=[("sq", s_)])
            p.mm([lambda e, c=c: e.matmul(ps[7][:, 0:TT], ones_bf, sq[s_][:, c, :], start=(c == 0), stop=(c == 7))
                  for c in range(8)], reads=[("sq", s_), "ones"], writes=[("ps", 7)])

        def stats2(t):
            s_ = t % 2
            p.op("act", lambda e: e.activation(out=tA[s_], in_=ps[7][:, 0:TT], func=ACT.Ln, scale=1.0 / D_MODEL, bias=EPS),
                 reads=[("ps", 7)], writes=[("tA", s_)])
            p.op("act", lambda e: e.activation(out=rstd[s_], in_=tA[s_], func=ACT.Exp, scale=-0.5),
                 reads=[("tA", s_)], writes=[("rstd", s_)])

        def affine(t):
            s_ = t % 2
            for c in range(8):
                b_ = cnt["tb"] % 4
                cnt["tb"] += 1
                p.op("dve", lambda e, c=c, b_=b_: e.tensor_tensor(tB[b_], xT[:, c, tsl(t)], rstd[s_], ALU.mult),
                     reads=[("xT", c, t), ("rstd", s_)], writes=[("tB", b_)])
                gsc = m["gs"][k][:, c:c + 1]
                shc = m["mod"][:, 3 * k * 8 + c:3 * k * 8 + c + 1]
                if c % 2 == 1:
                    p.op("pool", lambda e, c=c, b_=b_, gsc=gsc, shc=shc: e.tensor_scalar(
                        hT[:, c, tsl(t)], tB[b_], gsc, shc, ALU.mult, ALU.add),
                        reads=[("tB", b_), ("gs", l, k), ("mod", l)], writes=[("hT", t, c)])
                else:
                    p.op("act", lambda e, c=c, b_=b_, gsc=gsc, shc=shc: e.activation(
                        out=hT[:, c, tsl(t)], in_=tB[b_], func=ACT.Identity, scale=gsc, bias=shc),
                        reads=[("tB", b_), ("gs", l, k), ("mod", l)], writes=[("hT", t, c)])

        stats1(0)
        stats2(0)
        for t in range(NT):
            if t + 1 < NT:
                stats1(t + 1)
            affine(t)
            if t + 1 < NT:
                stats2(t + 1)

    def proj_mm(slot_ap, skey, src, src_keys, bank, t, nk=8):
        p.mm([lambda e, kc=kc: e.matmul(ps[bank][:, 0:TT], slot_ap[:, kc, :], src[:, kc, tsl(t)],
                                        start=(kc == 0), stop=(kc == nk - 1)) for kc in range(nk)],
             reads=[skey] + src_keys, writes=[("ps", bank)])

    cnt = dict(a=0, b=0, d=0, k=0, stg=0, dbank=0, tb=0, cv=0)

    def load_w(slots, name, w_ap, idx):
        s = cnt[name] % len(slots)
        cnt[name] += 1
        p.dma("pool", lambda e: e.dma_start(out=slots[s], in_=w_ap[idx].rearrange("q (k c) -> q k c", c=128)),
              (name, s), writes=[(name, s)])
        return slots[s], (name, s)

    def ffn(l, j):
        k = 0 if j == 0 else 2
        wg = din("wg%d%d" % (l, j), [NF, 128, 8 * 128])
        wu = din("wu%d%d" % (l, j), [NF, 128, 8 * 128])
        wd = din("wd%d%d" % (l, j), [8, 128, NF * 128])
        mk = ar.mark()
        hT = ar.alloc([8, NTOK], BF16)
        aT = ar.alloc([NF, 3 * TT], BF16)
        gd = M[l]["gd"][k]
        items = []
        for half in range(2):
            items += [("gu", half, f) for f in range(NF)] + [("d", half, d) for d in range(8)]
        loaded, nxt = {}, [0]

        def ensure(upto):
            while nxt[0] <= min(upto, len(items) - 1):
                it = items[nxt[0]]
                if it[0] == "gu":
                    loaded[nxt[0]] = (load_w(wA, "a", wg, it[2]), load_w(wB, "b", wu, it[2]))
                else:
                    loaded[nxt[0]] = (load_w(wD, "d", wd, it[2]),)
                nxt[0] += 1

        ensure(2)
        norm(l, k, hT)
        for i, it in enumerate(items):
            ensure(i + 2)
            tiles = [it[1] * 3 + x for x in range(3)]
            if it[0] == "gu":
                f = it[2]
                (a_ap, a_key), (b_ap, b_key) = loaded.pop(i)
                for ti, t in enumerate(tiles):
                    q = cnt["k"] % 2
                    cnt["k"] += 1
                    proj_mm(a_ap, a_key, hT, [("hT", t, c_) for c_ in range(8)], q, t)
                    proj_mm(b_ap, b_key, hT, [("hT", t, c_) for c_ in range(8)], 2 + q, t)
                    p.op("act", lambda e, q=q: e.activation(out=sg[q], in_=ps[q][:, 0:TT], func=ACT.Silu),
                         reads=[("ps", q)], writes=[("sg", q)])
                    p.op("dve", lambda e, q=q, f=f, ti=ti: e.tensor_tensor(
                        aT[:, f, ti * TT:(ti + 1) * TT], sg[q], ps[2 + q][:, 0:TT], ALU.mult),
                        reads=[("sg", q), ("ps", 2 + q)], writes=[("aT", f, ti)])
            else:
                d = it[2]
                ((d_ap, d_key),) = loaded.pop(i)
                for ti, t in enumerate(tiles):
                    bank = 4 + cnt["dbank"] % 3
                    cnt["dbank"] += 1
                    p.mm([lambda e, f=f, ti=ti, bank=bank, d_ap=d_ap: e.matmul(
                        ps[bank][:, 0:TT], d_ap[:, f, :], aT[:, f, ti * TT:(ti + 1) * TT],
                        start=(f == 0), stop=(f == NF - 1)) for f in range(NF)],
                        reads=[d_key] + [("aT", f, ti) for f in range(NF)], writes=[("ps", bank)])
                    p.op("dve", lambda e, d=d, t=t, bank=bank: e.scalar_tensor_tensor(
                        xT[:, d, tsl(t)], ps[bank][:, 0:TT], gd[:, d:d + 1], xT[:, d, tsl(t)], ALU.mult, ALU.add),
                        reads=[("ps", bank), ("xT", d, t), ("gd", l, k)], writes=[("xT", d, t)])
        p.barrier()
        ar.reset(mk)

    def inproj(l):
        win = din("win%d" % l, [16, 128, 8 * 128])
        qkg_in = din("qkg%d" % l, [128, 2])
        poolw_in = din("poolw%d" % l, [128, 4 * 128])
        pscale_in = din("pscale%d" % l, [128, 4])
        icnt_in = din("icnt", [128, 64])
        q_out = dout("q_out", [512, NTOK], BF16)
        k_out = dout("k_out", [512, NTOK], BF16)
        v_out = dout("v_out", [512, NTOK], BF16)
        p_out = dout("p_out", [512, NTOK], BF16)
        mk = ar.mark()
        hT = ar.alloc([8, NTOK], BF16)
        uT = ar.alloc([4, NTOK], F32)
        qkg = ar.alloc([2], F32)
        poolw32 = ar.alloc([4, 128], F32)
        poolw = ar.alloc([4, 128], BF16)
        pscale = ar.alloc([4], F32)
        icnt = ar.alloc([4, 16], F32)
        p.dma("sp", lambda e: e.dma_start(out=qkg, in_=qkg_in), ("i1", l), writes=["qkg"])
        p.dma("sp", lambda e: e.dma_start(out=poolw32, in_=poolw_in.rearrange("q (g c) -> q g c", g=4)), ("i2", l), writes=["poolw32"])
        p.dma("sp", lambda e: e.dma_start(out=pscale, in_=pscale_in), ("i3", l), writes=["pscale"])
        p.dma("sp", lambda e: e.dma_start(out=icnt, in_=icnt_in.rearrange("q (g c) -> q g c", g=4)), ("i4", l), writes=["icnt"])
        p.op("dve", lambda e: e.tensor_copy(poolw, poolw32), reads=["poolw32"], writes=["poolw"])
        gq8 = ar.alloc([2], F32)
        p.op("dve", lambda e: e.tensor_scalar(gq8[:, 0:1], qkg[:, 0:1], 0.125, None, ALU.mult), reads=["qkg"], writes=["gq8"])
        p.op("dve", lambda e: e.tensor_copy(gq8[:, 1:2], qkg[:, 1:2]), reads=["qkg", "gq8"], writes=["gq8"])
        norm(l, 1, hT)
        work = [(oc, t) for oc in range(8) for t in range(NT)]
        wl = {}

        def P_(i):
            oc, t = work[i]
            if t == 0:
                wl[oc] = load_w(wA, "a", win, oc)
            proj_mm(wl[oc][0], wl[oc][1], hT, [("hT", t, c_) for c_ in range(8)], i % 2, t)

        def Q_(i):
            q = i % 2
            p.op("act", lambda e: e.activation(out=sq[q][:, 0, :], in_=ps[q][:, 0:TT], func=ACT.Square),
                 reads=[("ps", q)], writes=[("sq", q)])

        def B_(i):
            q = i % 2
            p.mm([lambda e: e.matmul(ps[2 + q][:, 0:TT], bd_bf, sq[q][:, 0, :], start=True, stop=True)],
                 reads=[("sq", q), "bd"], writes=[("ps", 2 + q)])

        def L_(i):
            q = i % 2
            p.op("act", lambda e: e.activation(out=tA[q], in_=ps[2 + q][:, 0:TT], func=ACT.Ln, scale=1.0 / 64, bias=EPS),
                 reads=[("ps", 2 + q)], writes=[("tA", q)])
            p.op("act", lambda e: e.activation(out=rstd[q], in_=tA[q], func=ACT.Exp, scale=-0.5),
                 reads=[("tA", q)], writes=[("rstd", q)])

        def D_(i):
            oc, t = work[i]
            q = i % 2
            si = cnt["stg"] % 2
            cnt["stg"] += 1
            sb = stg[si].bitcast(BF16)[:, 0:TT]
            col = 0 if oc < 4 else 1
            p.op("dve", lambda e: e.scalar_tensor_tensor(sb, ps[q][:, 0:TT], gq8[:, col:col + 1], rstd[q], ALU.mult, ALU.mult),
                 reads=[("ps", q), ("rstd", q), "gq8"], writes=[("stg", si)])
            dst = (q_out if oc < 4 else k_out)[(oc % 4) * 128:(oc % 4 + 1) * 128, tsl(t)]
            p.dma("sp", lambda e: e.dma_start(out=dst, in_=sb), ("stg", si), reads=[("stg", si)])

        nw = len(work)
        P_(0)
        Q_(0)
        for i in range(nw):
            if i + 1 < nw:
                P_(i + 1)
            B_(i)
            if i + 1 < nw:
                Q_(i + 1)
            L_(i)
            D_(i)
        for oc in range(8, 16):
            w_ap, w_key = load_w(wA, "a", win, oc)
            for t in range(NT):
                q = cnt["k"] % 2
                cnt["k"] += 1
                proj_mm(w_ap, w_key, hT, [("hT", t, c_) for c_ in range(8)], q, t)
                if oc < 8:
                    pass
                elif oc < 12:
                    si = cnt["stg"] % 2
                    cnt["stg"] += 1
                    sb = stg[si].bitcast(BF16)[:, 0:TT]
                    p.op("act", lambda e, q=q, sb=sb: e.copy(sb, ps[q][:, 0:TT]), reads=[("ps", q)], writes=[("stg", si)])
                    dst = v_out[(oc - 8) * 128:(oc - 7) * 128, tsl(t)]
                    p.dma("sp", lambda e, sb=sb, dst=dst: e.dma_start(out=dst, in_=sb), ("stg", si), reads=[("stg", si)])
                else:
                    p.op("act", lambda e, q=q, oc=oc, t=t: e.copy(uT[:, oc - 12, tsl(t)], ps[q][:, 0:TT]),
                         reads=[("ps", q)], writes=[("uT", oc - 12)])
        p.barrier()
        pa = hT[:, 0:2, :].rearrange("q a t -> q (a t)").bitcast(F32)
        pb = hT[:, 2:4, :].rearrange("q a t -> q (a t)").bitcast(F32)
        dT = hT[:, 4:8, :]
        for g, w in enumerate(POOL_W):
            u = uT[:, g, :]
            p.op("dve", lambda e, u=u: e.tensor_tensor(u[:, 0:TT], u[:, 0:TT], valid, ALU.mult), reads=[("uT", g), "valid"], writes=[("uT", g)])
            src, step, bufs, bi = u, 1, [pa, pb], 0
            while step < w:
                dstb = bufs[bi]
                p.op("dve", lambda e, dstb=dstb, src=src, step=step: e.tensor_tensor(
                    dstb[:, step:NTOK], src[:, step:NTOK], src[:, 0:NTOK - step], ALU.add),
                    reads=[("uT", g), "pa", "pb"], writes=["pa" if bi == 0 else "pb"])
                p.op("dve", lambda e, dstb=dstb, src=src, step=step: e.tensor_copy(dstb[:, 0:step], src[:, 0:step]),
                     reads=[("uT", g), "pa", "pb"], writes=["pa" if bi == 0 else "pb"])
                src, step, bi = dstb, step * 2, 1 - bi
            S = src
            p.op("dve", lambda e, S=S, u=u, g=g, w=w: e.scalar_tensor_tensor(
                dT[:, g, :], S, 1.0 / w, u, ALU.mult, ALU.subtract), reads=["pa", "pb", ("uT", g)], writes=[("dT", g)])
            p.op("dve", lambda e, S=S, g=g: e.tensor_tensor(tA[0][:, 0:16], S[:, HALO:HALO + 16], icnt[:, g, :], ALU.mult),
                 reads=["pa", "pb", "icnt"], writes=["tA"])
            p.op("dve", lambda e, u=u, g=g: e.tensor_tensor(dT[:, g, HALO:HALO + 16], tA[0][:, 0:16], u[:, HALO:HALO + 16], ALU.subtract),
                 reads=["tA", ("uT", g)], writes=[("dT", g)])
            for t in range(NT):
                q = cnt["k"] % 2
                cnt["k"] += 1
                p.mm([lambda e, q=q, g=g, t=t: e.matmul(ps[q][:, 0:TT], poolw[:, g, :], dT[:, g, tsl(t)], start=True, stop=True)],
                     reads=["poolw", ("dT", g)], writes=[("ps", q)])
                si = cnt["stg"] % 2
                cnt["stg"] += 1
                sb = stg[si].bitcast(BF16)[:, 0:TT]
                p.op("act", lambda e, q=q, sb=sb, g=g: e.activation(out=sb, in_=ps[q][:, 0:TT], func=ACT.Copy, scale=pscale[:, g:g + 1]),
                     reads=[("ps", q), "pscale"], writes=[("stg", si)])
                dst = p_out[g * 128:(g + 1) * 128, tsl(t)]
                p.dma("sp", lambda e, sb=sb, dst=dst: e.dma_start(out=dst, in_=sb), ("stg", si), reads=[("stg", si)])
        p.barrier()
        ar.reset(mk)

    def out_proj(l, wname, srcT, src_keys):
        wout = din(wname, [8, 128, 8 * 128])
        g2 = M[l]["gd"][1]
        for d in range(8):
            w_ap, w_key = load_w(wA, "a", wout, d)
            for t in range(NT):
                q = cnt["k"] % 2
                cnt["k"] += 1
                proj_mm(w_ap, w_key, srcT, src_keys(t), q, t)
                p.op("dve", lambda e, q=q, d=d, t=t: e.scalar_tensor_tensor(
                    xT[:, d, tsl(t)], ps[q][:, 0:TT], g2[:, d:d + 1], xT[:, d, tsl(t)], ALU.mult, ALU.add),
                    reads=[("ps", q), ("xT", d, t), ("gd", l, 1)], writes=[("xT", d, t)])

    def mixout(l):
        o_in = din("o_in", [512, NTOK], BF16)
        p_in = din("p_in", [512, NTOK], BF16)
        mk = ar.mark()
        opT = ar.alloc([8, NTOK], BF16)
        p.dma("sp", lambda e: e.dma_start(out=opT[:, 0:4, :], in_=o_in.rearrange("(c q) t -> q c t", q=128)), "oin", writes=["opT0"])
        p.dma("sp", lambda e: e.dma_start(out=opT[:, 4:8, :], in_=p_in.rearrange("(c q) t -> q c t", q=128)), "pin", writes=["opT1"])
        out_proj(l, "wout%d" % l, opT, lambda t: ["opT0", "opT1"])
        p.barrier()
        ar.reset(mk)

    def conv(l):
        win = din("win%d" % l, [24, 128, 8 * 128])
        cw_in = din("convw%d" % l, [128, 24])
        mk = ar.mark()
        hT = ar.alloc([8, NTOK], BF16)
        yT = ar.alloc([8, NTOK], BF16)
        NB = 3
        vb = [ar.alloc([TT + 2], F32) for _ in range(NB)]
        yb = [ar.alloc([TT], F32) for _ in range(NB)]
        tC = [ar.alloc([TT], F32) for _ in range(NB)]
        bS = [ar.alloc([TT], F32) for _ in range(NB)]
        cw = ar.alloc([8, 3], F32)
        p.dma("sp", lambda e: e.dma_start(out=cw, in_=cw_in.rearrange("q (c j) -> q c j", j=3)), ("cw", l), writes=["cw"])
        witems = [(fc, kind) for fc in range(8) for kind in range(3)]
        wl, nxt = {}, [0]

        def ensure(upto):
            while nxt[0] <= min(upto, len(witems) - 1):
                fc, kind = witems[nxt[0]]
                wl[nxt[0]] = load_w(wA if kind != 1 else wB, "a" if kind != 1 else "b", win, kind * 8 + fc)
                nxt[0] += 1

        ensure(2)
        norm(l, 1, hT)
        for fc in range(8):
            ensure(3 * fc + 4)
            (wb_ap, wb_key), (wc_ap, wc_key), (wx_ap, wx_key) = wl.pop(3 * fc), wl.pop(3 * fc + 1), wl.pop(3 * fc + 2)
            for t in range(NT):
                q = cnt["k"] % 2
                cnt["k"] += 1
                r = cnt["cv"] % NB
                rp = (cnt["cv"] - 1) % NB
                cnt["cv"] += 1
                v_, vp_, y_ = vb[r], vb[rp], yb[r]
                proj_mm(wb_ap, wb_key, hT, [("hT", t, c_) for c_ in range(8)], q, t)
                proj_mm(wc_ap, wc_key, hT, [("hT", t, c_) for c_ in range(8)], 2 + q, t)
                proj_mm(wx_ap, wx_key, hT, [("hT", t, c_) for c_ in range(8)], 4 + q, t)
                p.op("act", lambda e, q=q, r=r: e.copy(tC[r], ps[2 + q][:, 0:TT]), reads=[("ps", 2 + q)], writes=[("tC", r)])
                p.op("act", lambda e, q=q, r=r: e.copy(bS[r], ps[q][:, 0:TT]), reads=[("ps", q)], writes=[("bS", r)])
                if t == 0:
                    p.op("pool", lambda e, v_=v_: e.memset(v_[:, 0:2], 0.0), reads=[("vb", r)], writes=[("vb", r)])
                else:
                    p.op("pool", lambda e, v_=v_, vp_=vp_: e.tensor_copy(v_[:, 0:2], vp_[:, TT:TT + 2]),
                         reads=[("vb", rp), ("vb", r)], writes=[("vb", r)])
                p.op("dve", lambda e, q=q, r=r, v_=v_: e.tensor_tensor(v_[:, 2:TT + 2], tC[r], ps[4 + q][:, 0:TT], ALU.mult),
                     reads=[("tC", r), ("ps", 4 + q), ("vb", r)], writes=[("vb", r)])
                if t == 0:
                    p.op("dve", lambda e, v_=v_: e.tensor_tensor(v_[:, 2:TT + 2], v_[:, 2:TT + 2], valid, ALU.mult),
                         reads=[("vb", r), "valid"], writes=[("vb", r)])
                p.op("act", lambda e, fc=fc, v_=v_, y_=y_: e.activation(out=y_, in_=v_[:, 2:TT + 2], func=ACT.Identity, scale=cw[:, fc, 2:3]),
                     reads=[("vb", r), "cw", ("yb", r)], writes=[("yb", r)])
                p.op("dve", lambda e, fc=fc, v_=v_, y_=y_: e.scalar_tensor_tensor(y_, v_[:, 1:TT + 1], cw[:, fc, 1:2], y_, ALU.mult, ALU.add),
                     reads=[("vb", r), "cw", ("yb", r)], writes=[("yb", r)])
                p.op("dve", lambda e, fc=fc, v_=v_, y_=y_: e.scalar_tensor_tensor(y_, v_[:, 0:TT], cw[:, fc, 0:1], y_, ALU.mult, ALU.add),
                     reads=[("vb", r), "cw", ("yb", r)], writes=[("yb", r)])
                p.op("dve", lambda e, fc=fc, t=t, r=r, y_=y_: e.tensor_tensor(yT[:, fc, tsl(t)], y_, bS[r], ALU.mult),
                     reads=[("yb", r), ("bS", r)], writes=[("yT", fc)])
        out_proj(l, "wout%d" % l, yT, lambda t: [("yT", fc) for fc in range(8)])
        p.barrier()
        ar.reset(mk)

    for l in layers:
        mods(l)
    for st in stages:
        if st[0] == "ffn":
            ffn(st[1], st[2])
        elif st[0] == "inproj":
            inproj(st[1])
        elif st[0] == "mixout":
            mixout(st[1])
        elif st[0] == "conv":
            conv(st[1])
    x_out = dout("x_out", [D_MODEL, NTOK])
    p.dma("sp", lambda e: e.dma_start(out=x_out.rearrange("(c q) t -> q c t", q=128), in_=xT), "xo",
          reads=[("xT", c, t) for c in range(8) for t in range(NT)])
    p.emit()
    es.close()
    return nc, names_in, names_out


def _tile_w(W, ncol=128):
    K, N = W.shape
    return np.ascontiguousarray(
        W.reshape(K // 128, 128, N // ncol, ncol).transpose(2, 1, 0, 3).reshape(N // ncol, 128, (K // 128) * ncol))


class host_prep:
    def __init__(self, inp):
        self.inp = inp
        self.cache = {}

    def w(self, name):
        if name in self.cache:
            return self.cache[name]
        I = self.inp
        if name.startswith("wada"):
            v = _tile_w(np.asarray(I["w_ada"][int(name[4:])]), 1024)
        elif name.startswith("bada"):
            v = np.ascontiguousarray(np.asarray(I["b_ada"][int(name[4:])]).reshape(72, 128).T)
        elif name.startswith("ng"):
            v = np.ascontiguousarray(np.asarray(I["norm_g"][int(name[2:])]).reshape(24, 128).T)
        elif name[:2] in ("wg", "wu", "wd"):
            l, j = int(name[2]), int(name[3])
            v = _tile_w(np.asarray(I["ffn_" + name[:2]][l, j]))
        elif name.startswith("win"):
            l = int(name[3:])
            v = _tile_w(np.asarray(I["w_in_ab"][l // 2] if l % 2 == 0 else I["w_in_c"][l // 2]))
        elif name.startswith("wout"):
            l = int(name[4:])
            v = _tile_w(np.asarray(I["w_out_ab"][l // 2] if l % 2 == 0 else I["w_out_c"][l // 2]))
        elif name.startswith("qkg"):
            g = np.asarray(I["qk_norm_g"][int(name[3:]) // 2])
            v = np.ascontiguousarray(np.stack([np.tile(g[0], 2), np.tile(g[1], 2)], axis=1))
        elif name.startswith("poolw"):
            pw = np.asarray(I["pool_w"][int(name[5:]) // 2])
            v = np.ascontiguousarray(pw.transpose(1, 0, 2).reshape(128, 4 * 128))
        elif name.startswith("pscale"):
            v = np.ascontiguousarray(np.asarray(I["pool_scale"][int(name[6:]) // 2]).reshape(4, 128).T)
        elif name.startswith("convw"):
            cw = np.asarray(I["conv_w"][int(name[5:]) // 2])
            v = np.ascontiguousarray(cw.reshape(3, 8, 128).transpose(2, 1, 0).reshape(128, 24))
        else:
            raise KeyError(name)
        self.cache[name] = v
        return v

    def t_inputs(self, core, names, x_full=None, carry=None):
        b, j = core // 4, core % 4
        m = {}
        for n in names:
            if n == "x_in":
                if carry is not None:
                    m[n] = carry[core]["x_out"]
                else:
                    xs = np.zeros((NTOK, D_MODEL), np.float32)
                    lo = j * TOK - HALO
                    src = np.asarray(x_full[b, max(lo, 0):(j + 1) * TOK])
                    xs[NTOK - src.shape[0]:] = src
                    m[n] = np.ascontiguousarray(xs.T)
            elif n == "valid":
                v = np.ones((128, TT), np.float32)
                if j == 0:
                    v[:, :HALO] = 0.0
                m[n] = v
            elif n == "icnt":
                ic = np.zeros((128, 4, 16), np.float32)
                for g, w in enumerate(POOL_W):
                    for t in range(16):
                        ic[:, g, t] = np.float32(1.0) / np.float32(min(t + 1, w) if j == 0 else w)
                m[n] = ic.reshape(128, 64)
            elif n == "cT":
                m[n] = np.ascontiguousarray(np.asarray(self.inp["c"][b]).reshape(8, 128).T)
            elif n in ("o_in", "p_in") or (n.startswith("mod") and n[3:].isdigit()):
                m[n] = carry[core][n]
            elif n == "wadax":
                blk = [i for i in range(27) if i % 4 == j]
                w = np.zeros((7, 128, 8 * 1024), np.float32)
                for pos, i in enumerate(blk):
                    w[pos] = self.w("wada%d" % (1 + i // 9))[i % 9]
                m[n] = w
            elif n == "badax":
                blk = [i for i in range(27) if i % 4 == j]
                bb = np.zeros((128, 56), np.float32)
                for pos, i in enumerate(blk):
                    bb[:, pos * 8:(pos + 1) * 8] = self.w("bada%d" % (1 + i // 9))[:, (i % 9) * 8:(i % 9 + 1) * 8]
                m[n] = bb
            else:
                m[n] = self.w(n)
        return m


QT = 512
NQT = SEQ // QT
NKB = SEQ // 128


def build_B(l):
    nc = bass.Bass("TRN2", target_bir_lowering=False)
    es = contextlib.ExitStack()
    p = Prog(nc, es)
    ar = Arena(nc, es, 30000)
    ps = [es.enter_context(nc.psum_tensor("ps%d" % i, [128, 512], F32)) for i in range(8)]
    q_in = nc.dram_tensor("qT", [128, SEQ], BF16, kind="ExternalInput").ap()
    k_in = nc.dram_tensor("kT", [128, SEQ], BF16, kind="ExternalInput").ap()
    v_in = nc.dram_tensor("v", [128, NKB * 128], BF16, kind="ExternalInput").ap()
    lam_in = nc.dram_tensor("lam", [128, 256], F32, kind="ExternalInput").ap()
    sg_in = nc.dram_tensor("sg", [128, 1], F32, kind="ExternalInput").ap()
    tri_in = nc.dram_tensor("tri", [128, 128], F32, kind="ExternalInput").ap()
    o_out = nc.dram_tensor("oT", [128, SEQ], BF16, kind="ExternalOutput").ap()

    qT = ar.alloc([SEQ], BF16)
    kT = ar.alloc([SEQ], BF16)
    V = ar.alloc([NKB, 128], BF16)
    ones_bf = ar.alloc([128], BF16)
    tri32 = ar.alloc([128], F32)
    tri = ar.alloc([128], BF16)
    P1 = [ar.alloc([QT], BF16) for _ in range(2)]
    P2 = [ar.alloc([QT], BF16) for _ in range(2)]
    LA = [[ar.alloc([QT], F32) for _ in range(2)] for _ in range(2)]
    Lh = [ar.alloc([QT], BF16) for _ in range(2)]
    Ll = [ar.alloc([QT], BF16) for _ in range(2)]
    lam = ar.alloc([256], F32)
    prod = ar.alloc([128], F32)
    sc = ar.alloc([8], F32)
    sgv = ar.alloc([1], F32)
    lnv = ar.alloc([QT], F32)
    rl = [ar.alloc([QT], F32) for _ in range(2)]
    t1 = [ar.alloc([QT], F32) for _ in range(2)]
    t2 = ar.alloc([QT], F32)
    rs = ar.alloc([QT], F32)
    sqo = ar.alloc([QT], BF16)
    ob = [ar.alloc([QT], BF16) for _ in range(2)]

    for i in range(4):
        sl = slice(i * 2048, (i + 1) * 2048)
        p.dma("sp", lambda e, sl=sl: e.dma_start(out=qT[:, sl], in_=q_in[:, sl]), ("q", i), writes=[("q", i)])
        p.dma("sp", lambda e, sl=sl: e.dma_start(out=kT[:, sl], in_=k_in[:, sl]), ("k", i), writes=[("k", i)])
        p.dma("sp", lambda e, i=i: e.dma_start(out=V[:, i * 16:(i + 1) * 16, :],
                                               in_=v_in[:, i * 2048:(i + 1) * 2048].rearrange("q (b d) -> q b d", d=128)),
              ("v", i), writes=[("v", i)])
    p.dma("sp", lambda e: e.dma_start(out=lam, in_=lam_in), "c1", writes=["lam"])
    p.dma("sp", lambda e: e.dma_start(out=sgv, in_=sg_in), "c2", writes=["sgv"])
    p.dma("sp", lambda e: e.dma_start(out=tri32, in_=tri_in), "c3", writes=["tri32"])
    p.op("dve", lambda e: e.memset(ones_bf, 1.0), writes=["ones"])
    p.op("dve", lambda e: e.tensor_copy(tri, tri32), reads=["tri32"], writes=["tri"])
    li = LAM_INIT[l]
    for j in range(2):
        p.op("dve", lambda e, j=j: e.tensor_tensor(prod[:, 0:64], lam[:, j * 128:j * 128 + 64], lam[:, j * 128 + 64:j * 128 + 128], ALU.mult),
             reads=["lam"], writes=["prod"])
        p.op("dve", lambda e, j=j: e.reduce_sum(sc[:, j:j + 1], prod[:, 0:64], axis=mybir.AxisListType.X),
             reads=["prod"], writes=[("sc", j)])
        p.op("act", lambda e, j=j: e.activation(out=sc[:, 2 + j:3 + j], in_=sc[:, j:j + 1], func=ACT.Exp),
             reads=[("sc", j)], writes=[("sc", 2 + j)])
    p.op("dve", lambda e: e.tensor_tensor(sc[:, 4:5], sc[:, 3:4], sc[:, 2:3], ALU.subtract),
         reads=[("sc", 2), ("sc", 3)], writes=[("sc", 4)])
    p.op("dve", lambda e: e.tensor_scalar(sc[:, 5:6], sc[:, 4:5], -li, None, ALU.add), reads=[("sc", 4)], writes=["nlam"])
    p.op("dve", lambda e: e.tensor_scalar(sc[:, 6:7], sgv, 1.0 - li, None, ALU.mult), reads=["sgv"], writes=["sgs"])
    nlam = sc[:, 5:6]
    sgs = sc[:, 6:7]

    steps = [(qt, kb) for qt in range(NQT) for kb in range(4 * qt + 4)]

    def geom(n):
        qt, kb = steps[n]
        i = kb - 4 * qt
        return qt, kb, i, max(0, i) * 128, n % 2

    def qk(n):
        qt, kb, i, c0, b = geom(n)
        q0 = qt * QT
        kcols = slice(kb * 128, (kb + 1) * 128)
        rk = [("k", kb // 16), ("q", qt // 4)]
        p.mm([lambda e: e.matmul(ps[b][:, c0:QT], kT[0:64, kcols], qT[0:64, q0 + c0:q0 + QT], start=True, stop=True),
              lambda e: e.matmul(ps[2 + b][:, c0:QT], kT[64:128, kcols], qT[64:128, q0 + c0:q0 + QT], start=True, stop=True)],
             reads=rk, writes=[("ps", b), ("ps", 2 + b)])

    def softmax_pv(n):
        qt, kb, i, c0, b = geom(n)
        par = qt % 2
        nkb = 4 * qt + 4
        p.op("act", lambda e: e.activation(out=P1[b][:, c0:QT], in_=ps[b][:, c0:QT], func=ACT.Exp),
             reads=[("ps", b)], writes=[("P1", b)])
        p.op("act", lambda e: e.activation(out=P2[b][:, c0:QT], in_=ps[2 + b][:, c0:QT], func=ACT.Exp),
             reads=[("ps", 2 + b)], writes=[("P2", b)])
        if i >= 0:
            p.op("dve", lambda e: e.tensor_tensor(P1[b][:, c0:c0 + 128], P1[b][:, c0:c0 + 128], tri, ALU.mult),
                 reads=[("P1", b), "tri"], writes=[("P1", b)])
            p.op("pool", lambda e: e.tensor_tensor(P2[b][:, c0:c0 + 128], P2[b][:, c0:c0 + 128], tri, ALU.mult),
                 reads=[("P2", b), "tri"], writes=[("P2", b)])
        st, sp_ = (kb == 0), (kb == nkb - 1)
        p.mm([lambda e: e.matmul(ps[4 + par][:, c0:QT], V[:, kb, :], P1[b][:, c0:QT], start=st, stop=sp_),
              lambda e: e.matmul(ps[6 + par][:, c0:QT], V[:, kb, :], P2[b][:, c0:QT], start=st, stop=sp_)],
             reads=[("P1", b), ("P2", b), ("v", kb // 16)], writes=[("ps", 4 + par), ("ps", 6 + par)])
        for m_, Pm in ((0, P1), (1, P2)):
            pk = ("P%d" % (m_ + 1), b)
            for eng, lo, hi, part in (("dve", c0, QT, 0),):
                if lo >= hi:
                    continue
                if kb == 0:
                    p.op(eng, lambda e, m_=m_, Pm=Pm, lo=lo, hi=hi: e.tensor_copy(LA[m_][par][:, lo:hi], Pm[b][:, lo:hi]),
                         reads=[pk], writes=[("LA", m_, par, part)])
                else:
                    p.op(eng, lambda e, m_=m_, Pm=Pm, lo=lo, hi=hi: e.tensor_tensor(
                        LA[m_][par][:, lo:hi], LA[m_][par][:, lo:hi], Pm[b][:, lo:hi], ALU.add),
                        reads=[pk, ("LA", m_, par, part)], writes=[("LA", m_, par, part)])

    def fin1(qt, b):
        par = qt % 2
        for m_ in range(2):
            p.op("dve", lambda e, m_=m_: e.tensor_copy(Lh[m_], LA[m_][par]),
                 reads=[("LA", m_, par, 0)], writes=[("Lh", m_)])
            p.op("dve", lambda e, m_=m_: e.tensor_tensor(Ll[m_], LA[m_][par], Lh[m_], ALU.subtract),
                 reads=[("LA", m_, par, 0), ("Lh", m_)], writes=[("Ll", m_)])
            bank = b + 2 * m_
            p.mm([lambda e, m_=m_, bank=bank: e.matmul(ps[bank][:, 0:QT], ones_bf, Lh[m_], start=True, stop=False),
                  lambda e, m_=m_, bank=bank: e.matmul(ps[bank][:, 0:QT], ones_bf, Ll[m_], start=False, stop=True)],
                 reads=[("Lh", m_), ("Ll", m_), "ones"], writes=[("ps", bank)])
            p.op("act", lambda e, bank=bank: e.activation(out=lnv, in_=ps[bank][:, 0:QT], func=ACT.Ln),
                 reads=[("ps", bank)], writes=["lnv"])
            p.op("act", lambda e, m_=m_: e.activation(out=rl[m_], in_=lnv, func=ACT.Exp, scale=-1.0),
                 reads=["lnv"], writes=[("rl", m_)])
        p.op("dve", lambda e: e.tensor_tensor(t1[par], ps[4 + par][:, 0:QT], rl[0], ALU.mult),
             reads=[("ps", 4 + par), ("rl", 0)], writes=[("t1", par)])
        p.op("dve", lambda e: e.tensor_tensor(t2, ps[6 + par][:, 0:QT], rl[1], ALU.mult),
             reads=[("ps", 6 + par), ("rl", 1)], writes=["t2"])
        p.op("dve", lambda e: e.scalar_tensor_tensor(t1[par], t2, nlam, t1[par], ALU.mult, ALU.add),
             reads=["t2", ("t1", par), "nlam"], writes=[("t1", par)])
        p.op("act", lambda e: e.activation(out=sqo, in_=t1[par], func=ACT.Square), reads=[("t1", par)], writes=["sqo"])
        p.mm([lambda e: e.matmul(ps[4 + par][:, 0:QT], ones_bf, sqo, start=True, stop=True)],
             reads=["sqo", "ones"], writes=[("ps", 4 + par)])

    def fin2(qt):
        par = qt % 2
        q0 = qt * QT
        p.op("act", lambda e: e.activation(out=lnv, in_=ps[4 + par][:, 0:QT], func=ACT.Ln, scale=1.0 / 128, bias=EPS),
             reads=[("ps", 4 + par)], writes=["lnv"])
        p.op("act", lambda e: e.activation(out=rs, in_=lnv, func=ACT.Exp, scale=-0.5), reads=["lnv"], writes=["rs"])
        o_b = ob[par]
        p.op("dve", lambda e: e.scalar_tensor_tensor(o_b, t1[par], sgs, rs, ALU.mult, ALU.mult),
             reads=[("t1", par), "sgs", "rs"], writes=[("ob", par)])
        p.dma("sp", lambda e: e.dma_start(out=o_out[:, q0:q0 + QT], in_=o_b), ("ob", par), reads=[("ob", par)])

    N = len(steps)
    pend1, pend2 = [], []
    qk(0)
    for n in range(N):
        if n + 1 < N:
            qk(n + 1)
        softmax_pv(n)
        for (due, qt_) in list(pend2):
            if due <= n:
                fin2(qt_)
                pend2.remove((due, qt_))
        for (due, qt_) in list(pend1):
            if due <= n:
                fin1(qt_, n % 2)
                pend1.remove((due, qt_))
                pend2.append((n + 1, qt_))
        qt, kb = steps[n]
        if kb == 4 * qt + 3:
            pend1.append((n + 1, qt))
    for (due, qt_) in pend2:
        fin2(qt_)
    for (due, qt_) in pend1:
        fin1(qt_, N % 2)
        fin2(qt_)
    p.emit()
    es.close()
    return nc


def _b_inputs(H, rT, l, core):
    b, h = core // 4, core % 4
    I = H.inp
    e = l // 2
    rows = slice(h * 128, (h + 1) * 128)
    qT = np.concatenate([rT[b * 4 + j]["q_out"][rows, HALO:] for j in range(4)], axis=1)
    kT = np.concatenate([rT[b * 4 + j]["k_out"][rows, HALO:] for j in range(4)], axis=1)
    vT = np.concatenate([rT[b * 4 + j]["v_out"][rows, HALO:] for j in range(4)], axis=1)
    v = np.ascontiguousarray(vT.T.reshape(NKB, 128, 128).transpose(1, 0, 2).reshape(128, NKB * 128))
    lam = np.concatenate([np.asarray(I[n][e]) for n in ("lambda_q1", "lambda_k1", "lambda_q2", "lambda_k2")])
    lam = np.ascontiguousarray(np.broadcast_to(lam[None, :], (128, 256))).astype(np.float32)
    sg = np.ascontiguousarray(np.asarray(I["subln_g"][e]).reshape(128, 1))
    tri = (np.arange(128)[:, None] <= np.arange(128)[None, :]).astype(np.float32)
    return {"qT": np.ascontiguousarray(qT), "kT": np.ascontiguousarray(kT), "v": v, "lam": lam, "sg": sg, "tri": tri}


def _o_for(rB, core):
    b, j = core // 4, core % 4
    lo = j * TOK - HALO
    o = np.zeros((512, NTOK), dtype=rB[0]["oT"].dtype)
    for h in range(4):
        src = rB[b * 4 + h]["oT"][:, max(lo, 0):(j + 1) * TOK]
        o[h * 128:(h + 1) * 128, NTOK - src.shape[1]:] = src
    return o


def _run(nc, in_maps):
    return run_bass_kernel_spmd(nc, in_maps, core_ids=list(range(N_CORES))).results


def _assemble_mods(rA):
    mods = {}
    for b in range(BATCH):
        tabs = {l: np.zeros((128, 72), np.float32) for l in (1, 2, 3)}
        for i in range(27):
            j, pos = i % 4, i // 4
            tabs[1 + i // 9][:, (i % 9) * 8:(i % 9 + 1) * 8] = rA[b * 4 + j]["modx_out"][:, pos * 8:(pos + 1) * 8]
        for j in range(4):
            mods[b * 4 + j] = dict(mod0=rA[b * 4 + j]["mod0_out"], mod1=tabs[1], mod2=tabs[2], mod3=tabs[3])
    return mods


def kernel(**inputs):
    H = host_prep(inputs)
    cores = range(N_CORES)
    nc, nin, _ = build_T([("ffn", 0, 0), ("inproj", 0)], True)
    rA = _run(nc, [H.t_inputs(c, nin, x_full=inputs["x"]) for c in cores])
    mods = _assemble_mods(rA)
    rB = _run(build_B(0), [_b_inputs(H, rA, 0, c) for c in cores])
    carry = [dict(x_out=rA[c]["x_out"], o_in=_o_for(rB, c), p_in=rA[c]["p_out"], **mods[c]) for c in cores]
    nc, nin, _ = build_T([("mixout", 0), ("ffn", 0, 1), ("ffn", 1, 0), ("conv", 1), ("ffn", 1, 1),
                          ("ffn", 2, 0), ("inproj", 2)], False)
    rC = _run(nc, [H.t_inputs(c, nin, carry=carry) for c in cores])
    del rA, carry
    rB = _run(build_B(2), [_b_inputs(H, rC, 2, c) for c in cores])
    carry = [dict(x_out=rC[c]["x_out"], o_in=_o_for(rB, c), p_in=rC[c]["p_out"], **mods[c]) for c in cores]
    nc, nin, _ = build_T([("mixout", 2), ("ffn", 2, 1), ("ffn", 3, 0), ("conv", 3), ("ffn", 3, 1)], False)
    rD = _run(nc, [H.t_inputs(c, nin, carry=carry) for c in cores])
    out = np.empty((BATCH, SEQ, D_MODEL), np.float32)
    for c in cores:
        b, j = c // 4, c % 4
        out[b, j * TOK:(j + 1) * TOK] = rD[c]["x_out"][:, HALO:].T
    return out
```

```python
import numpy as np
import concourse.bass as bass
import concourse.mybir as mybir
from concourse.bass_utils import run_bass_kernel_spmd

F32 = mybir.dt.float32
BF16 = mybir.dt.bfloat16
ACT = mybir.ActivationFunctionType
ALU = mybir.AluOpType


import contextlib
import ml_dtypes

N_CORES = 8
D_MODEL, BATCH, SEQ, DEPTH = 1024, 2, 8192, 4
D_FF = 2816
NF = D_FF // 128
TOK = SEQ * BATCH // N_CORES
HALO = 64
NTOK = TOK + HALO
TT = 352
NT = NTOK // TT
EPS = 1e-6
LAM_INIT = {0: 0.8 - 0.6 * float(np.exp(-0.3 * 0)), 2: 0.8 - 0.6 * float(np.exp(-0.3 * 2))}
POOL_W = (2, 4, 8, 16)


class Prog:
    def __init__(self, nc, es):
        self.nc, self.es = nc, es
        self.eng = dict(pe=nc.tensor, act=nc.scalar, dve=nc.vector, pool=nc.gpsimd, sp=nc.sync)
        self.rec = {k: [] for k in self.eng}
        self.sem = {k: es.enter_context(nc.semaphore("s_" + k)) for k in ("pe", "act", "dve", "pool")}
        self.cnt = {k: 0 for k in self.sem}
        self.dsem = {}
        self.waited = {k: {} for k in self.eng}
        self.lastw, self.rd = {}, {}

    def _deps(self, e, reads, writes, nowaw=False):
        toks = []
        for r in reads:
            t = self.lastw.get(r)
            if t:
                toks.append(t)
        for w in writes:
            t = self.lastw.get(w)
            if t and not nowaw:
                toks.append(t)
            toks.extend(self.rd.get(w, {}).values())
        waits = {}
        for (s, v, te) in toks:
            if te == "pe" and e == "pe":
                continue
            k = id(s)
            if self.waited[e].get(k, 0) >= v:
                continue
            if k not in waits or waits[k][1] < v:
                waits[k] = (s, v)
        for k, (s, v) in waits.items():
            self.waited[e][k] = v
        return list(waits.values())

    def _commit(self, tok, reads, writes):
        for w in writes:
            self.lastw[w] = tok
            self.rd[w] = {}
        for r in reads:
            d = self.rd.setdefault(r, {})
            k = id(tok[0])
            if k not in d or d[k][1] < tok[1]:
                d[k] = tok

    def op(self, e, fn, reads=(), writes=()):
        waits = self._deps(e, reads, writes)
        self.cnt[e] += 1
        tok = (self.sem[e], self.cnt[e], e)
        self.rec[e].append((waits, fn, (self.sem[e], 1)))
        self._commit(tok, reads, writes)

    def mm(self, fns, reads=(), writes=()):
        waits = self._deps("pe", reads, writes)
        self.cnt["pe"] += 1
        tok = (self.sem["pe"], self.cnt["pe"], "pe")
        n = len(fns)
        for i, fn in enumerate(fns):
            self.rec["pe"].append((waits if i == 0 else [], fn, (self.sem["pe"], 1) if i == n - 1 else None))
        self._commit(tok, reads, writes)

    def dma(self, q, fn, key, reads=(), writes=(), nowaw=False):
        if key not in self.dsem:
            self.dsem[key] = [self.es.enter_context(self.nc.semaphore("d%d" % len(self.dsem))), 0]
        waits = self._deps(q, reads, writes, nowaw)
        d = self.dsem[key]
        d[1] += 16
        tok = (d[0], d[1], None)
        self.rec[q].append((waits, fn, (d[0], 16)))
        self._commit(tok, reads, writes)

    def barrier(self):
        for e in self.eng:
            waits = []
            for k, s in self.sem.items():
                if k != e and self.cnt[k] > self.waited[e].get(id(s), 0):
                    waits.append((s, self.cnt[k]))
                    self.waited[e][id(s)] = self.cnt[k]
                if k == e and e != "pe" and self.cnt[k] > self.waited[e].get(id(s), 0):
                    waits.append((s, self.cnt[k]))
                    self.waited[e][id(s)] = self.cnt[k]
            for d in self.dsem.values():
                if d[1] > self.waited[e].get(id(d[0]), 0):
                    waits.append((d[0], d[1]))
                    self.waited[e][id(d[0])] = d[1]
            if waits:
                self.rec[e].append((waits, None, None))
        self.lastw, self.rd = {}, {}

    def emit(self):
        self.barrier()
        with self.nc.Block() as block:
            for name, dec in (("pe", block.tensor), ("act", block.scalar), ("dve", block.vector),
                              ("pool", block.gpsimd), ("sp", block.sync)):
                recs = self.rec[name]

                def body(eng, recs=recs):
                    for waits, fn, inc in recs:
                        for s, v in waits:
                            eng.wait_ge(s, v)
                        if fn is None:
                            continue
                        ins = fn(eng)
                        if inc is not None:
                            ins.then_inc(inc[0], inc[1])
                dec(body)


class Arena:
    def __init__(self, nc, es, words):
        self.t = es.enter_context(nc.sbuf_tensor("arena", [128, words], F32))
        self.words, self.off = words, 0

    def mark(self):
        return self.off

    def reset(self, m):
        self.off = m

    def alloc(self, shape_free, dtype):
        n = int(np.prod(shape_free))
        w = n if dtype == F32 else (n + 1) // 2
        w = (w + 7) // 8 * 8
        assert self.off + w <= self.words, ("SBUF arena overflow", self.off, w, self.words)
        ap = self.t[:, self.off:self.off + w]
        self.off += w
        if dtype != F32:
            ap = ap.bitcast(dtype)
        ap = ap[:, 0:n]
        if len(shape_free) == 2:
            ap = ap.rearrange("p (a b) -> p a b", a=shape_free[0])
        elif len(shape_free) == 3:
            ap = ap.rearrange("p (a b c) -> p a b c", a=shape_free[0], b=shape_free[1])
        return ap


def tsl(t):
    return slice(t * TT, (t + 1) * TT)


def build_T(stages, first):
    nc = bass.Bass("TRN2", target_bir_lowering=False)
    es = contextlib.ExitStack()
    p = Prog(nc, es)
    ar = Arena(nc, es, 53000)
    ps = [es.enter_context(nc.psum_tensor("ps%d" % i, [128, 512], F32)) for i in range(8)]
    names_in, names_out = [], []

    def din(name, shape, dt=F32):
        names_in.append(name)
        return nc.dram_tensor(name, list(shape), dt, kind="ExternalInput").ap()

    def dout(name, shape, dt=F32):
        names_out.append(name)
        return nc.dram_tensor(name, list(shape), dt, kind="ExternalOutput").ap()

    layers = sorted({s[1] for s in stages})
    xT = ar.alloc([8, NTOK], F32)
    ones_bf = ar.alloc([128], BF16)
    bd_bf = ar.alloc([128], BF16)
    valid = ar.alloc([TT], F32)
    wA = [ar.alloc([8, 128], BF16) for _ in range(3)]
    wB = [ar.alloc([8, 128], BF16) for _ in range(3)]
    wD = [ar.alloc([NF, 128], BF16) for _ in range(3)]
    sq = [ar.alloc([8, TT], BF16) for _ in range(2)]
    tA = [ar.alloc([TT], F32) for _ in range(2)]
    rstd = [ar.alloc([TT], F32) for _ in range(2)]
    tB = [ar.alloc([TT], F32) for _ in range(4)]
    sg = [ar.alloc([TT], F32) for _ in range(2)]
    stg = sg
    cT = ar.alloc([8], F32)
    cact = ar.alloc([8], BF16)
    M = {}
    for l in layers:
        M[l] = dict(mod=ar.alloc([72], F32), bada=ar.alloc([72], F32), ng=ar.alloc([24], F32),
                    gs=[ar.alloc([8], F32) for _ in range(3)], gd=[ar.alloc([8], F32) for _ in range(3)])
    phase_mark = ar.mark()

    x_in = din("x_in", [D_MODEL, NTOK])
    valid_in = din("valid", [128, TT])
    c_in = din("cT", [128, 8])

    p.op("dve", lambda e: e.memset(ones_bf, 1.0), writes=["ones"])
    p.op("dve", lambda e: e.memset(bd_bf, 0.0), writes=["bd"])
    p.op("dve", lambda e: e.memset(bd_bf[0:64, 0:64], 1.0), writes=["bd"])
    p.op("dve", lambda e: e.memset(bd_bf[64:128, 64:128], 1.0), writes=["bd"])
    for c in range(8):
        p.dma("sp", lambda e, c=c: e.dma_start(out=xT[:, c, :], in_=x_in[c * 128:(c + 1) * 128, :]), ("x", c),
              writes=[("xT", c, t) for t in range(NT)])
    p.dma("sp", lambda e: e.dma_start(out=valid, in_=valid_in), "cst", writes=["valid"])
    p.dma("sp", lambda e: e.dma_start(out=cT, in_=c_in), "cst2", writes=["cT"])
    p.op("act", lambda e: e.activation(out=cact, in_=cT, func=ACT.Silu), reads=["cT"], writes=["cact"])

    def mods_compute(wada, nblk, bada_sb, bada_key, out_sb, out_key):
        mk = ar.mark()
        wm = [ar.alloc([8, 1024], BF16) for _ in range(2)]
        for mi in range(nblk):
            s_ = mi % 2
            p.dma("pool", lambda e, s_=s_, mi=mi: e.dma_start(out=wm[s_], in_=wada[mi].rearrange("q (k c) -> q k c", k=8)),
                  ("wm", s_), writes=[("wm", s_)])
            fns = []
            for ch in range(8):
                for kc in range(8):
                    fns.append(lambda e, s_=s_, ch=ch, kc=kc, mi=mi: e.matmul(
                        ps[7][:, mi * 8 + ch:mi * 8 + ch + 1], wm[s_][:, kc, ch * 128:(ch + 1) * 128],
                        cact[:, kc:kc + 1], start=(kc == 0), stop=(kc == 7)))
            p.mm(fns, reads=[("wm", s_), "cact"], writes=[("ps", 7)])
        p.op("dve", lambda e: e.tensor_tensor(out_sb, ps[7][:, 0:nblk * 8], bada_sb, ALU.add),
             reads=[("ps", 7), bada_key], writes=[out_key])
        p.barrier()
        ar.reset(mk)

    def mods_derive(l):
        m = M[l]
        ng_in = din("ng%d" % l, [128, 24])
        p.dma("sp", lambda e: e.dma_start(out=m["ng"], in_=ng_in), ("c2", l), writes=[("ng", l)])
        for k in range(3):
            p.op("dve", lambda e, k=k: e.scalar_tensor_tensor(
                m["gs"][k], m["mod"][:, (3 * k + 1) * 8:(3 * k + 2) * 8], 1.0, m["ng"][:, k * 8:(k + 1) * 8],
                ALU.add, ALU.mult), reads=[("mod", l), ("ng", l)], writes=[("gs", l, k)])
            p.op("dve", lambda e, k=k: e.tensor_scalar(
                m["gd"][k], m["mod"][:, (3 * k + 2) * 8:(3 * k + 3) * 8], 0.5 if k != 1 else 1.0, None, ALU.mult),
                reads=[("mod", l)], writes=[("gd", l, k)])

    def mods(l):
        m = M[l]
        if first:
            wada = din("wada%d" % l, [9, 128, 8 * 1024])
            bada_in = din("bada%d" % l, [128, 72])
            p.dma("sp", lambda e: e.dma_start(out=m["bada"], in_=bada_in), ("c1", l), writes=[("bada", l)])
            mods_compute(wada, 9, m["bada"], ("bada", l), m["mod"], ("mod", l))
            mod_out = dout("mod%d_out" % l, [128, 72])
            p.dma("sp", lambda e: e.dma_start(out=mod_out, in_=m["mod"]), ("mo", l), reads=[("mod", l)])
            wadax = din("wadax", [7, 128, 8 * 1024])
            badax_in = din("badax", [128, 56])
            modx_out = dout("modx_out", [128, 56])
            p.dma("sp", lambda e: e.dma_start(out=m["bada"][:, 0:56], in_=badax_in), ("c1", l), reads=[("bada", l)], writes=[("bada", l)])
            mk = ar.mark()
            modx = ar.alloc([56], F32)
            mods_compute(wadax, 7, m["bada"][:, 0:56], ("bada", l), modx, "modx")
            p.dma("sp", lambda e: e.dma_start(out=modx_out, in_=modx), "mxo", reads=["modx"])
            p.barrier()
            ar.reset(mk)
        else:
            mod_in = din("mod%d" % l, [128, 72])
            p.dma("sp", lambda e: e.dma_start(out=m["mod"], in_=mod_in), ("c1", l), writes=[("mod", l)])
        mods_derive(l)

    def norm(l, k, hT):
        m = M[l]

        def stats1(t):
            s_ = t % 2
            p.op("act", lambda e: e.activation(out=sq[s_], in_=xT[:, :, tsl(t)], func=ACT.Square),
                 reads=[("xT", c, t) for c in range(8)], writes=[("sq", s_)])
            p.mm([lambda e, c=c: e.matmul(ps[7][:, 0:TT], ones_bf, sq[s_][:, c, :], start=(c == 0), stop=(c == 7))
                  for c in range(8)], reads=[("sq", s_), "ones"], writes=[("ps", 7)])

        def stats2(t):
            s_ = t % 2
            p.op("act", lambda e: e.activation(out=tA[s_], in_=ps[7][:, 0:TT], func=ACT.Ln, scale=1.0 / D_MODEL, bias=EPS),
                 reads=[("ps", 7)], writes=[("tA", s_)])
            p.op("act", lambda e: e.activation(out=rstd[s_], in_=tA[s_], func=ACT.Exp, scale=-0.5),
                 reads=[("tA", s_)], writes=[("rstd", s_)])

        def affine(t):
            s_ = t % 2
            for c in range(8):
                b_ = cnt["tb"] % 4
                cnt["tb"] += 1
                p.op("dve", lambda e, c=c, b_=b_: e.tensor_tensor(tB[b_], xT[:, c, tsl(t)], rstd[s_], ALU.mult),
                     reads=[("xT", c, t), ("rstd", s_)], writes=[("tB", b_)])
                gsc = m["gs"][k][:, c:c + 1]
                shc = m["mod"][:, 3 * k * 8 + c:3 * k * 8 + c + 1]
                if c in (1, 3, 4, 6, 7):
                    p.op("pool", lambda e, c=c, b_=b_, gsc=gsc, shc=shc: e.tensor_scalar(
                        hT[:, c, tsl(t)], tB[b_], gsc, shc, ALU.mult, ALU.add),
                        reads=[("tB", b_), ("gs", l, k), ("mod", l)], writes=[("hT", t, c)])
                else:
                    p.op("act", lambda e, c=c, b_=b_, gsc=gsc, shc=shc: e.activation(
                        out=hT[:, c, tsl(t)], in_=tB[b_], func=ACT.Identity, scale=gsc, bias=shc),
                        reads=[("tB", b_), ("gs", l, k), ("mod", l)], writes=[("hT", t, c)])

        stats1(0)
        stats2(0)
        for t in range(NT):
            if t + 1 < NT:
                stats1(t + 1)
            affine(t)
            if t + 1 < NT:
                stats2(t + 1)

    def proj_mm(slot_ap, skey, src, src_keys, bank, t, nk=8):
        p.mm([lambda e, kc=kc: e.matmul(ps[bank][:, 0:TT], slot_ap[:, kc, :], src[:, kc, tsl(t)],
                                        start=(kc == 0), stop=(kc == nk - 1)) for kc in range(nk)],
             reads=[skey] + src_keys, writes=[("ps", bank)])

    cnt = dict(a=0, b=0, d=0, k=0, stg=0, dbank=0, tb=0, cv=0)

    def load_w(slots, name, w_ap, idx):
        s = cnt[name] % len(slots)
        cnt[name] += 1
        p.dma("pool", lambda e: e.dma_start(out=slots[s], in_=w_ap[idx].rearrange("q (k c) -> q k c", c=128)),
              (name, s), writes=[(name, s)])
        return slots[s], (name, s)

    def ffn(l, j):
        k = 0 if j == 0 else 2
        wg = din("wg%d%d" % (l, j), [NF, 128, 8 * 128])
        wu = din("wu%d%d" % (l, j), [NF, 128, 8 * 128])
        wd = din("wd%d%d" % (l, j), [8, 128, NF * 128])
        mk = ar.mark()
        hT = ar.alloc([8, NTOK], BF16)
        aT = ar.alloc([NF, 3 * TT], BF16)
        gd = M[l]["gd"][k]
        items = []
        for half in range(2):
            items += [("gu", half, f) for f in range(NF)] + [("d", half, d) for d in range(8)]
        loaded, nxt = {}, [0]

        def ensure(upto):
            while nxt[0] <= min(upto, len(items) - 1):
                it = items[nxt[0]]
                if it[0] == "gu":
                    loaded[nxt[0]] = (load_w(wA, "a", wg, it[2]), load_w(wB, "b", wu, it[2]))
                else:
                    loaded[nxt[0]] = (load_w(wD, "d", wd, it[2]),)
                nxt[0] += 1

        ensure(2)
        norm(l, k, hT)
        for i, it in enumerate(items):
            ensure(i + 2)
            tiles = [it[1] * 3 + x for x in range(3)]
            if it[0] == "gu":
                f = it[2]
                (a_ap, a_key), (b_ap, b_key) = loaded.pop(i)
                for ti, t in enumerate(tiles):
                    q = cnt["k"] % 2
                    cnt["k"] += 1
                    proj_mm(a_ap, a_key, hT, [("hT", t, c_) for c_ in range(8)], q, t)
                    proj_mm(b_ap, b_key, hT, [("hT", t, c_) for c_ in range(8)], 2 + q, t)
                    p.op("act", lambda e, q=q: e.activation(out=sg[q], in_=ps[q][:, 0:TT], func=ACT.Silu),
                         reads=[("ps", q)], writes=[("sg", q)])
                    p.op("dve", lambda e, q=q, f=f, ti=ti: e.tensor_tensor(
                        aT[:, f, ti * TT:(ti + 1) * TT], sg[q], ps[2 + q][:, 0:TT], ALU.mult),
                        reads=[("sg", q), ("ps", 2 + q)], writes=[("aT", f, ti)])
            else:
                d = it[2]
                ((d_ap, d_key),) = loaded.pop(i)
                for ti, t in enumerate(tiles):
                    bank = 4 + cnt["dbank"] % 3
                    cnt["dbank"] += 1
                    p.mm([lambda e, f=f, ti=ti, bank=bank, d_ap=d_ap: e.matmul(
                        ps[bank][:, 0:TT], d_ap[:, f, :], aT[:, f, ti * TT:(ti + 1) * TT],
                        start=(f == 0), stop=(f == NF - 1)) for f in range(NF)],
                        reads=[d_key] + [("aT", f, ti) for f in range(NF)], writes=[("ps", bank)])
                    p.op("dve", lambda e, d=d, t=t, bank=bank: e.scalar_tensor_tensor(
                        xT[:, d, tsl(t)], ps[bank][:, 0:TT], gd[:, d:d + 1], xT[:, d, tsl(t)], ALU.mult, ALU.add),
                        reads=[("ps", bank), ("xT", d, t), ("gd", l, k)], writes=[("xT", d, t)])
        p.barrier()
        ar.reset(mk)

    def inproj(l):
        win = din("win%d" % l, [16, 128, 8 * 128])
        qkg_in = din("qkg%d" % l, [128, 2])
        poolw_in = din("poolw%d" % l, [128, 4 * 128])
        pscale_in = din("pscale%d" % l, [128, 4])
        icnt_in = din("icnt", [128, 64])
        q_out = dout("q_out", [512, NTOK], BF16)
        k_out = dout("k_out", [512, NTOK], BF16)
        v_out = dout("v_out", [512, NTOK], BF16)
        p_out = dout("p_out", [512, NTOK], BF16)
        mk = ar.mark()
        hT = ar.alloc([8, NTOK], BF16)
        uT = ar.alloc([4, NTOK], F32)
        qkg = ar.alloc([2], F32)
        poolw32 = ar.alloc([4, 128], F32)
        poolw = ar.alloc([4, 128], BF16)
        pscale = ar.alloc([4], F32)
        icnt = ar.alloc([4, 16], F32)
        p.dma("sp", lambda e: e.dma_start(out=qkg, in_=qkg_in), ("i1", l), writes=["qkg"])
        p.dma("sp", lambda e: e.dma_start(out=poolw32, in_=poolw_in.rearrange("q (g c) -> q g c", g=4)), ("i2", l), writes=["poolw32"])
        p.dma("sp", lambda e: e.dma_start(out=pscale, in_=pscale_in), ("i3", l), writes=["pscale"])
        p.dma("sp", lambda e: e.dma_start(out=icnt, in_=icnt_in.rearrange("q (g c) -> q g c", g=4)), ("i4", l), writes=["icnt"])
        p.op("dve", lambda e: e.tensor_copy(poolw, poolw32), reads=["poolw32"], writes=["poolw"])
        gq8 = ar.alloc([2], F32)
        p.op("dve", lambda e: e.tensor_scalar(gq8[:, 0:1], qkg[:, 0:1], 0.125, None, ALU.mult), reads=["qkg"], writes=["gq8"])
        p.op("dve", lambda e: e.tensor_copy(gq8[:, 1:2], qkg[:, 1:2]), reads=["qkg", "gq8"], writes=["gq8"])
        norm(l, 1, hT)
        work = [(oc, t) for oc in range(8) for t in range(NT)]
        wl = {}

        def P_(i):
            oc, t = work[i]
            if t == 0:
                wl[oc] = load_w(wA, "a", win, oc)
            proj_mm(wl[oc][0], wl[oc][1], hT, [("hT", t, c_) for c_ in range(8)], i % 2, t)

        def Q_(i):
            q = i % 2
            p.op("act", lambda e: e.activation(out=sq[q][:, 0, :], in_=ps[q][:, 0:TT], func=ACT.Square),
                 reads=[("ps", q)], writes=[("sq", q)])

        def B_(i):
            q = i % 2
            p.mm([lambda e: e.matmul(ps[2 + q][:, 0:TT], bd_bf, sq[q][:, 0, :], start=True, stop=True)],
                 reads=[("sq", q), "bd"], writes=[("ps", 2 + q)])

        def L_(i):
            q = i % 2
            p.op("act", lambda e: e.activation(out=tA[q], in_=ps[2 + q][:, 0:TT], func=ACT.Ln, scale=1.0 / 64, bias=EPS),
                 reads=[("ps", 2 + q)], writes=[("tA", q)])
            p.op("act", lambda e: e.activation(out=rstd[q], in_=tA[q], func=ACT.Exp, scale=-0.5),
                 reads=[("tA", q)], writes=[("rstd", q)])

        def D_(i):
            oc, t = work[i]
            q = i % 2
            si = cnt["stg"] % 2
            cnt["stg"] += 1
            sb = stg[si].bitcast(BF16)[:, 0:TT]
            col = 0 if oc < 4 else 1
            p.op("dve", lambda e: e.scalar_tensor_tensor(sb, ps[q][:, 0:TT], gq8[:, col:col + 1], rstd[q], ALU.mult, ALU.mult),
                 reads=[("ps", q), ("rstd", q), "gq8"], writes=[("stg", si)])
            dst = (q_out if oc < 4 else k_out)[(oc % 4) * 128:(oc % 4 + 1) * 128, tsl(t)]
            p.dma("sp", lambda e: e.dma_start(out=dst, in_=sb), ("stg", si), reads=[("stg", si)])

        nw = len(work)
        P_(0)
        Q_(0)
        for i in range(nw):
            if i + 1 < nw:
                P_(i + 1)
            B_(i)
            if i + 1 < nw:
                Q_(i + 1)
            L_(i)
            D_(i)
        for oc in range(8, 16):
            w_ap, w_key = load_w(wA, "a", win, oc)
            for t in range(NT):
                q = cnt["k"] % 2
                cnt["k"] += 1
                proj_mm(w_ap, w_key, hT, [("hT", t, c_) for c_ in range(8)], q, t)
                if oc < 8:
                    pass
                elif oc < 12:
                    si = cnt["stg"] % 2
                    cnt["stg"] += 1
                    sb = stg[si].bitcast(BF16)[:, 0:TT]
                    p.op("act", lambda e, q=q, sb=sb: e.copy(sb, ps[q][:, 0:TT]), reads=[("ps", q)], writes=[("stg", si)])
                    dst = v_out[(oc - 8) * 128:(oc - 7) * 128, tsl(t)]
                    p.dma("sp", lambda e, sb=sb, dst=dst: e.dma_start(out=dst, in_=sb), ("stg", si), reads=[("stg", si)])
                else:
                    p.op("act", lambda e, q=q, oc=oc, t=t: e.copy(uT[:, oc - 12, tsl(t)], ps[q][:, 0:TT]),
                         reads=[("ps", q)], writes=[("uT", oc - 12)])
        p.barrier()
        pa = hT[:, 0:2, :].rearrange("q a t -> q (a t)").bitcast(F32)
        pb = hT[:, 2:4, :].rearrange("q a t -> q (a t)").bitcast(F32)
        dT = hT[:, 4:8, :]
        for g, w in enumerate(POOL_W):
            u = uT[:, g, :]
            p.op("dve", lambda e, u=u: e.tensor_tensor(u[:, 0:TT], u[:, 0:TT], valid, ALU.mult), reads=[("uT", g), "valid"], writes=[("uT", g)])
            src, step, bufs, bi = u, 1, [pa, pb], 0
            while step < w:
                dstb = bufs[bi]
                p.op("dve", lambda e, dstb=dstb, src=src, step=step: e.tensor_tensor(
                    dstb[:, step:NTOK], src[:, step:NTOK], src[:, 0:NTOK - step], ALU.add),
                    reads=[("uT", g), "pa", "pb"], writes=["pa" if bi == 0 else "pb"])
                p.op("dve", lambda e, dstb=dstb, src=src, step=step: e.tensor_copy(dstb[:, 0:step], src[:, 0:step]),
                     reads=[("uT", g), "pa", "pb"], writes=["pa" if bi == 0 else "pb"])
                src, step, bi = dstb, step * 2, 1 - bi
            S = src
            p.op("dve", lambda e, S=S, u=u, g=g, w=w: e.scalar_tensor_tensor(
                dT[:, g, :], S, 1.0 / w, u, ALU.mult, ALU.subtract), reads=["pa", "pb", ("uT", g)], writes=[("dT", g)])
            p.op("dve", lambda e, S=S, g=g: e.tensor_tensor(tA[0][:, 0:16], S[:, HALO:HALO + 16], icnt[:, g, :], ALU.mult),
                 reads=["pa", "pb", "icnt"], writes=["tA"])
            p.op("dve", lambda e, u=u, g=g: e.tensor_tensor(dT[:, g, HALO:HALO + 16], tA[0][:, 0:16], u[:, HALO:HALO + 16], ALU.subtract),
                 reads=["tA", ("uT", g)], writes=[("dT", g)])
            for t in range(NT):
                q = cnt["k"] % 2
                cnt["k"] += 1
                p.mm([lambda e, q=q, g=g, t=t: e.matmul(ps[q][:, 0:TT], poolw[:, g, :], dT[:, g, tsl(t)], start=True, stop=True)],
                     reads=["poolw", ("dT", g)], writes=[("ps", q)])
                si = cnt["stg"] % 2
                cnt["stg"] += 1
                sb = stg[si].bitcast(BF16)[:, 0:TT]
                p.op("act", lambda e, q=q, sb=sb, g=g: e.activation(out=sb, in_=ps[q][:, 0:TT], func=ACT.Copy, scale=pscale[:, g:g + 1]),
                     reads=[("ps", q), "pscale"], writes=[("stg", si)])
                dst = p_out[g * 128:(g + 1) * 128, tsl(t)]
                p.dma("sp", lambda e, sb=sb, dst=dst: e.dma_start(out=dst, in_=sb), ("stg", si), reads=[("stg", si)])
        p.barrier()
        ar.reset(mk)

    def out_proj(l, wname, srcT, src_keys):
        wout = din(wname, [8, 128, 8 * 128])
        g2 = M[l]["gd"][1]
        for d in range(8):
            w_ap, w_key = load_w(wA, "a", wout, d)
            for t in range(NT):
                q = cnt["k"] % 2
                cnt["k"] += 1
                proj_mm(w_ap, w_key, srcT, src_keys(t), q, t)
                p.op("dve", lambda e, q=q, d=d, t=t: e.scalar_tensor_tensor(
                    xT[:, d, tsl(t)], ps[q][:, 0:TT], g2[:, d:d + 1], xT[:, d, tsl(t)], ALU.mult, ALU.add),
                    reads=[("ps", q), ("xT", d, t), ("gd", l, 1)], writes=[("xT", d, t)])

    def mixout(l):
        o_in = din("o_in", [512, NTOK], BF16)
        p_in = din("p_in", [512, NTOK], BF16)
        mk = ar.mark()
        opT = ar.alloc([8, NTOK], BF16)
        for t in range(NT):
            p.dma("sp", lambda e, t=t: e.dma_start(out=opT[:, 0:4, tsl(t)], in_=o_in.rearrange("(c q) t -> q c t", q=128)[:, :, tsl(t)]),
                  ("oin", t), writes=[("opT0", t)])
            p.dma("sp", lambda e, t=t: e.dma_start(out=opT[:, 4:8, tsl(t)], in_=p_in.rearrange("(c q) t -> q c t", q=128)[:, :, tsl(t)]),
                  ("pin", t), writes=[("opT1", t)])
        out_proj(l, "wout%d" % l, opT, lambda t: [("opT0", t), ("opT1", t)])
        p.barrier()
        ar.reset(mk)

    def conv(l):
        win = din("win%d" % l, [24, 128, 8 * 128])
        cw_in = din("convw%d" % l, [128, 24])
        mk = ar.mark()
        hT = ar.alloc([8, NTOK], BF16)
        yT = ar.alloc([8, NTOK], BF16)
        NB = 3
        vb = [ar.alloc([TT + 2], F32) for _ in range(NB)]
        yb = [ar.alloc([TT], F32) for _ in range(NB)]
        tC = [ar.alloc([TT], F32) for _ in range(NB)]
        bS = [ar.alloc([TT], F32) for _ in range(NB)]
        cw = ar.alloc([8, 3], F32)
        p.dma("sp", lambda e: e.dma_start(out=cw, in_=cw_in.rearrange("q (c j) -> q c j", j=3)), ("cw", l), writes=["cw"])
        witems = [(fc, kind) for fc in range(8) for kind in range(3)]
        wl, nxt = {}, [0]

        def ensure(upto):
            while nxt[0] <= min(upto, len(witems) - 1):
                fc, kind = witems[nxt[0]]
                wl[nxt[0]] = load_w(wA if kind != 1 else wB, "a" if kind != 1 else "b", win, kind * 8 + fc)
                nxt[0] += 1

        ensure(2)
        norm(l, 1, hT)
        for fc in range(8):
            ensure(3 * fc + 4)
            (wb_ap, wb_key), (wc_ap, wc_key), (wx_ap, wx_key) = wl.pop(3 * fc), wl.pop(3 * fc + 1), wl.pop(3 * fc + 2)
            for t in range(NT):
                q = cnt["k"] % 2
                cnt["k"] += 1
                r = cnt["cv"] % NB
                rp = (cnt["cv"] - 1) % NB
                cnt["cv"] += 1
                v_, vp_, y_ = vb[r], vb[rp], yb[r]
                proj_mm(wb_ap, wb_key, hT, [("hT", t, c_) for c_ in range(8)], q, t)
                proj_mm(wc_ap, wc_key, hT, [("hT", t, c_) for c_ in range(8)], 2 + q, t)
                proj_mm(wx_ap, wx_key, hT, [("hT", t, c_) for c_ in range(8)], 4 + q, t)
                p.op("act", lambda e, q=q, r=r: e.copy(tC[r], ps[2 + q][:, 0:TT]), reads=[("ps", 2 + q)], writes=[("tC", r)])
                p.op("act", lambda e, q=q, r=r: e.copy(bS[r], ps[q][:, 0:TT]), reads=[("ps", q)], writes=[("bS", r)])
                if t == 0:
                    p.op("pool", lambda e, v_=v_: e.memset(v_[:, 0:2], 0.0), reads=[("vb", r)], writes=[("vb", r)])
                else:
                    p.op("pool", lambda e, v_=v_, vp_=vp_: e.tensor_copy(v_[:, 0:2], vp_[:, TT:TT + 2]),
                         reads=[("vb", rp), ("vb", r)], writes=[("vb", r)])
                p.op("dve", lambda e, q=q, r=r, v_=v_: e.tensor_tensor(v_[:, 2:TT + 2], tC[r], ps[4 + q][:, 0:TT], ALU.mult),
                     reads=[("tC", r), ("ps", 4 + q), ("vb", r)], writes=[("vb", r)])
                if t == 0:
                    p.op("dve", lambda e, v_=v_: e.tensor_tensor(v_[:, 2:TT + 2], v_[:, 2:TT + 2], valid, ALU.mult),
                         reads=[("vb", r), "valid"], writes=[("vb", r)])
                p.op("act", lambda e, fc=fc, v_=v_, y_=y_: e.activation(out=y_, in_=v_[:, 2:TT + 2], func=ACT.Identity, scale=cw[:, fc, 2:3]),
                     reads=[("vb", r), "cw", ("yb", r)], writes=[("yb", r)])
                p.op("dve", lambda e, fc=fc, v_=v_, y_=y_: e.scalar_tensor_tensor(y_, v_[:, 1:TT + 1], cw[:, fc, 1:2], y_, ALU.mult, ALU.add),
                     reads=[("vb", r), "cw", ("yb", r)], writes=[("yb", r)])
                p.op("dve", lambda e, fc=fc, v_=v_, y_=y_: e.scalar_tensor_tensor(y_, v_[:, 0:TT], cw[:, fc, 0:1], y_, ALU.mult, ALU.add),
                     reads=[("vb", r), "cw", ("yb", r)], writes=[("yb", r)])
                p.op("dve", lambda e, fc=fc, t=t, r=r, y_=y_: e.tensor_tensor(yT[:, fc, tsl(t)], y_, bS[r], ALU.mult),
                     reads=[("yb", r), ("bS", r)], writes=[("yT", fc)])
        out_proj(l, "wout%d" % l, yT, lambda t: [("yT", fc) for fc in range(8)])
        p.barrier()
        ar.reset(mk)

    for l in layers:
        mods(l)
    for st in stages:
        if st[0] == "ffn":
            ffn(st[1], st[2])
        elif st[0] == "inproj":
            inproj(st[1])
        elif st[0] == "mixout":
            mixout(st[1])
        elif st[0] == "conv":
            conv(st[1])
    x_out = dout("x_out", [D_MODEL, NTOK])
    p.dma("sp", lambda e: e.dma_start(out=x_out.rearrange("(c q) t -> q c t", q=128), in_=xT), "xo",
          reads=[("xT", c, t) for c in range(8) for t in range(NT)])
    p.emit()
    es.close()
    return nc, names_in, names_out


def _tile_w(W, ncol=128):
    K, N = W.shape
    return np.ascontiguousarray(
        W.reshape(K // 128, 128, N // ncol, ncol).transpose(2, 1, 0, 3).reshape(N // ncol, 128, (K // 128) * ncol))


class host_prep:
    def __init__(self, inp):
        self.inp = inp
        self.cache = {}

    def w(self, name):
        if name in self.cache:
            return self.cache[name]
        I = self.inp
        if name.startswith("wada"):
            v = _tile_w(np.asarray(I["w_ada"][int(name[4:])]), 1024)
        elif name.startswith("bada"):
            v = np.ascontiguousarray(np.asarray(I["b_ada"][int(name[4:])]).reshape(72, 128).T)
        elif name.startswith("ng"):
            v = np.ascontiguousarray(np.asarray(I["norm_g"][int(name[2:])]).reshape(24, 128).T)
        elif name[:2] in ("wg", "wu", "wd"):
            l, j = int(name[2]), int(name[3])
            v = _tile_w(np.asarray(I["ffn_" + name[:2]][l, j]))
        elif name.startswith("win"):
            l = int(name[3:])
            v = _tile_w(np.asarray(I["w_in_ab"][l // 2] if l % 2 == 0 else I["w_in_c"][l // 2]))
        elif name.startswith("wout"):
            l = int(name[4:])
            v = _tile_w(np.asarray(I["w_out_ab"][l // 2] if l % 2 == 0 else I["w_out_c"][l // 2]))
        elif name.startswith("qkg"):
            g = np.asarray(I["qk_norm_g"][int(name[3:]) // 2])
            v = np.ascontiguousarray(np.stack([np.tile(g[0], 2), np.tile(g[1], 2)], axis=1))
        elif name.startswith("poolw"):
            pw = np.asarray(I["pool_w"][int(name[5:]) // 2])
            v = np.ascontiguousarray(pw.transpose(1, 0, 2).reshape(128, 4 * 128))
        elif name.startswith("pscale"):
            v = np.ascontiguousarray(np.asarray(I["pool_scale"][int(name[6:]) // 2]).reshape(4, 128).T)
        elif name.startswith("convw"):
            cw = np.asarray(I["conv_w"][int(name[5:]) // 2])
            v = np.ascontiguousarray(cw.reshape(3, 8, 128).transpose(2, 1, 0).reshape(128, 24))
        else:
            raise KeyError(name)
        self.cache[name] = v
        return v

    def t_inputs(self, core, names, x_full=None, carry=None):
        b, j = core // 4, core % 4
        m = {}
        for n in names:
            if n == "x_in":
                if carry is not None:
                    m[n] = carry[core]["x_out"]
                else:
                    xs = np.zeros((NTOK, D_MODEL), np.float32)
                    lo = j * TOK - HALO
                    src = np.asarray(x_full[b, max(lo, 0):(j + 1) * TOK])
                    xs[NTOK - src.shape[0]:] = src
                    m[n] = np.ascontiguousarray(xs.T)
            elif n == "valid":
                v = np.ones((128, TT), np.float32)
                if j == 0:
                    v[:, :HALO] = 0.0
                m[n] = v
            elif n == "icnt":
                ic = np.zeros((128, 4, 16), np.float32)
                for g, w in enumerate(POOL_W):
                    for t in range(16):
                        ic[:, g, t] = np.float32(1.0) / np.float32(min(t + 1, w) if j == 0 else w)
                m[n] = ic.reshape(128, 64)
            elif n == "cT":
                m[n] = np.ascontiguousarray(np.asarray(self.inp["c"][b]).reshape(8, 128).T)
            elif n in ("o_in", "p_in") or (n.startswith("mod") and n[3:].isdigit()):
                m[n] = carry[core][n]
            elif n == "wadax":
                blk = [i for i in range(27) if i % 4 == j]
                w = np.zeros((7, 128, 8 * 1024), np.float32)
                for pos, i in enumerate(blk):
                    w[pos] = self.w("wada%d" % (1 + i // 9))[i % 9]
                m[n] = w
            elif n == "badax":
                blk = [i for i in range(27) if i % 4 == j]
                bb = np.zeros((128, 56), np.float32)
                for pos, i in enumerate(blk):
                    bb[:, pos * 8:(pos + 1) * 8] = self.w("bada%d" % (1 + i // 9))[:, (i % 9) * 8:(i % 9 + 1) * 8]
                m[n] = bb
            else:
                m[n] = self.w(n)
        return m


QT = 512
NQT = SEQ // QT
NKB = SEQ // 128


def build_B(l):
    nc = bass.Bass("TRN2", target_bir_lowering=False)
    es = contextlib.ExitStack()
    p = Prog(nc, es)
    ar = Arena(nc, es, 30000)
    ps = [es.enter_context(nc.psum_tensor("ps%d" % i, [128, 512], F32)) for i in range(8)]
    q_in = nc.dram_tensor("qT", [128, SEQ], BF16, kind="ExternalInput").ap()
    k_in = nc.dram_tensor("kT", [128, SEQ], BF16, kind="ExternalInput").ap()
    v_in = nc.dram_tensor("v", [128, NKB * 128], BF16, kind="ExternalInput").ap()
    lam_in = nc.dram_tensor("lam", [128, 256], F32, kind="ExternalInput").ap()
    sg_in = nc.dram_tensor("sg", [128, 1], F32, kind="ExternalInput").ap()
    tri_in = nc.dram_tensor("tri", [128, 128], F32, kind="ExternalInput").ap()
    o_out = nc.dram_tensor("oT", [128, SEQ], BF16, kind="ExternalOutput").ap()

    qT = ar.alloc([SEQ], BF16)
    kT = ar.alloc([SEQ], BF16)
    V = ar.alloc([NKB, 128], BF16)
    ones_bf = ar.alloc([128], BF16)
    tri32 = ar.alloc([128], F32)
    tri = ar.alloc([128], BF16)
    P1 = [ar.alloc([QT], BF16) for _ in range(2)]
    P2 = [ar.alloc([QT], BF16) for _ in range(2)]
    LA = [[ar.alloc([QT], F32) for _ in range(2)] for _ in range(2)]
    Lh = [ar.alloc([QT], BF16) for _ in range(2)]
    Ll = [ar.alloc([QT], BF16) for _ in range(2)]
    lam = ar.alloc([256], F32)
    prod = ar.alloc([128], F32)
    sc = ar.alloc([8], F32)
    sgv = ar.alloc([1], F32)
    lnv = ar.alloc([QT], F32)
    rl = [ar.alloc([QT], F32) for _ in range(2)]
    t1 = [ar.alloc([QT], F32) for _ in range(2)]
    t2 = ar.alloc([QT], F32)
    rs = ar.alloc([QT], F32)
    sqo = ar.alloc([QT], BF16)
    ob = [ar.alloc([QT], BF16) for _ in range(2)]

    for i in range(4):
        sl = slice(i * 2048, (i + 1) * 2048)
        p.dma("sp", lambda e, sl=sl: e.dma_start(out=qT[:, sl], in_=q_in[:, sl]), ("q", i), writes=[("q", i)])
        p.dma("sp", lambda e, sl=sl: e.dma_start(out=kT[:, sl], in_=k_in[:, sl]), ("k", i), writes=[("k", i)])
        p.dma("sp", lambda e, i=i: e.dma_start(out=V[:, i * 16:(i + 1) * 16, :],
                                               in_=v_in[:, i * 2048:(i + 1) * 2048].rearrange("q (b d) -> q b d", d=128)),
              ("v", i), writes=[("v", i)])
    p.dma("sp", lambda e: e.dma_start(out=lam, in_=lam_in), "c1", writes=["lam"])
    p.dma("sp", lambda e: e.dma_start(out=sgv, in_=sg_in), "c2", writes=["sgv"])
    p.dma("sp", lambda e: e.dma_start(out=tri32, in_=tri_in), "c3", writes=["tri32"])
    p.op("dve", lambda e: e.memset(ones_bf, 1.0), writes=["ones"])
    p.op("dve", lambda e: e.tensor_copy(tri, tri32), reads=["tri32"], writes=["tri"])
    li = LAM_INIT[l]
    for j in range(2):
        p.op("dve", lambda e, j=j: e.tensor_tensor(prod[:, 0:64], lam[:, j * 128:j * 128 + 64], lam[:, j * 128 + 64:j * 128 + 128], ALU.mult),
             reads=["lam"], writes=["prod"])
        p.op("dve", lambda e, j=j: e.reduce_sum(sc[:, j:j + 1], prod[:, 0:64], axis=mybir.AxisListType.X),
             reads=["prod"], writes=[("sc", j)])
        p.op("act", lambda e, j=j: e.activation(out=sc[:, 2 + j:3 + j], in_=sc[:, j:j + 1], func=ACT.Exp),
             reads=[("sc", j)], writes=[("sc", 2 + j)])
    p.op("dve", lambda e: e.tensor_tensor(sc[:, 4:5], sc[:, 3:4], sc[:, 2:3], ALU.subtract),
         reads=[("sc", 2), ("sc", 3)], writes=[("sc", 4)])
    p.op("dve", lambda e: e.tensor_scalar(sc[:, 5:6], sc[:, 4:5], -li, None, ALU.add), reads=[("sc", 4)], writes=["nlam"])
    p.op("dve", lambda e: e.tensor_scalar(sc[:, 6:7], sgv, 1.0 - li, None, ALU.mult), reads=["sgv"], writes=["sgs"])
    nlam = sc[:, 5:6]
    sgs = sc[:, 6:7]

    steps = [(qt, kb) for qt in range(NQT) for kb in range(4 * qt + 4)]

    def geom(n):
        qt, kb = steps[n]
        i = kb - 4 * qt
        return qt, kb, i, max(0, i) * 128, n % 2

    def qk(n):
        qt, kb, i, c0, b = geom(n)
        q0 = qt * QT
        kcols = slice(kb * 128, (kb + 1) * 128)
        rk = [("k", kb // 16), ("q", qt // 4)]
        p.mm([lambda e: e.matmul(ps[b][:, c0:QT], kT[0:64, kcols], qT[0:64, q0 + c0:q0 + QT], start=True, stop=True),
              lambda e: e.matmul(ps[2 + b][:, c0:QT], kT[64:128, kcols], qT[64:128, q0 + c0:q0 + QT], start=True, stop=True)],
             reads=rk, writes=[("ps", b), ("ps", 2 + b)])

    def softmax_pv(n):
        qt, kb, i, c0, b = geom(n)
        par = qt % 2
        nkb = 4 * qt + 4
        p.op("act", lambda e: e.activation(out=P1[b][:, c0:QT], in_=ps[b][:, c0:QT], func=ACT.Exp),
             reads=[("ps", b)], writes=[("P1", b)])
        p.op("act", lambda e: e.activation(out=P2[b][:, c0:QT], in_=ps[2 + b][:, c0:QT], func=ACT.Exp),
             reads=[("ps", 2 + b)], writes=[("P2", b)])
        if i >= 0:
            p.op("dve", lambda e: e.tensor_tensor(P1[b][:, c0:c0 + 128], P1[b][:, c0:c0 + 128], tri, ALU.mult),
                 reads=[("P1", b), "tri"], writes=[("P1", b)])
            p.op("pool", lambda e: e.tensor_tensor(P2[b][:, c0:c0 + 128], P2[b][:, c0:c0 + 128], tri, ALU.mult),
                 reads=[("P2", b), "tri"], writes=[("P2", b)])
        st, sp_ = (kb == 0), (kb == nkb - 1)
        p.mm([lambda e: e.matmul(ps[4 + par][:, c0:QT], V[:, kb, :], P1[b][:, c0:QT], start=st, stop=sp_),
              lambda e: e.matmul(ps[6 + par][:, c0:QT], V[:, kb, :], P2[b][:, c0:QT], start=st, stop=sp_)],
             reads=[("P1", b), ("P2", b), ("v", kb // 16)], writes=[("ps", 4 + par), ("ps", 6 + par)])
        for m_, Pm in ((0, P1), (1, P2)):
            pk = ("P%d" % (m_ + 1), b)
            for eng, lo, hi, part in (("dve", c0, QT, 0),):
                if lo >= hi:
                    continue
                if kb == 0:
                    p.op(eng, lambda e, m_=m_, Pm=Pm, lo=lo, hi=hi: e.tensor_copy(LA[m_][par][:, lo:hi], Pm[b][:, lo:hi]),
                         reads=[pk], writes=[("LA", m_, par, part)])
                else:
                    p.op(eng, lambda e, m_=m_, Pm=Pm, lo=lo, hi=hi: e.tensor_tensor(
                        LA[m_][par][:, lo:hi], LA[m_][par][:, lo:hi], Pm[b][:, lo:hi], ALU.add),
                        reads=[pk, ("LA", m_, par, part)], writes=[("LA", m_, par, part)])

    def fin1(qt, b):
        par = qt % 2
        for m_ in range(2):
            p.op("dve", lambda e, m_=m_: e.tensor_copy(Lh[m_], LA[m_][par]),
                 reads=[("LA", m_, par, 0)], writes=[("Lh", m_)])
            p.op("dve", lambda e, m_=m_: e.tensor_tensor(Ll[m_], LA[m_][par], Lh[m_], ALU.subtract),
                 reads=[("LA", m_, par, 0), ("Lh", m_)], writes=[("Ll", m_)])
            bank = b + 2 * m_
            p.mm([lambda e, m_=m_, bank=bank: e.matmul(ps[bank][:, 0:QT], ones_bf, Lh[m_], start=True, stop=False),
                  lambda e, m_=m_, bank=bank: e.matmul(ps[bank][:, 0:QT], ones_bf, Ll[m_], start=False, stop=True)],
                 reads=[("Lh", m_), ("Ll", m_), "ones"], writes=[("ps", bank)])
            p.op("act", lambda e, bank=bank: e.activation(out=lnv, in_=ps[bank][:, 0:QT], func=ACT.Ln),
                 reads=[("ps", bank)], writes=["lnv"])
            p.op("act", lambda e, m_=m_: e.activation(out=rl[m_], in_=lnv, func=ACT.Exp, scale=-1.0),
                 reads=["lnv"], writes=[("rl", m_)])
        p.op("dve", lambda e: e.tensor_tensor(t1[par], ps[4 + par][:, 0:QT], rl[0], ALU.mult),
             reads=[("ps", 4 + par), ("rl", 0)], writes=[("t1", par)])
        p.op("dve", lambda e: e.tensor_tensor(t2, ps[6 + par][:, 0:QT], rl[1], ALU.mult),
             reads=[("ps", 6 + par), ("rl", 1)], writes=["t2"])
        p.op("dve", lambda e: e.scalar_tensor_tensor(t1[par], t2, nlam, t1[par], ALU.mult, ALU.add),
             reads=["t2", ("t1", par), "nlam"], writes=[("t1", par)])
        p.op("act", lambda e: e.activation(out=sqo, in_=t1[par], func=ACT.Square), reads=[("t1", par)], writes=["sqo"])
        p.mm([lambda e: e.matmul(ps[4 + par][:, 0:QT], ones_bf, sqo, start=True, stop=True)],
             reads=["sqo", "ones"], writes=[("ps", 4 + par)])

    def fin2(qt):
        par = qt % 2
        q0 = qt * QT
        p.op("act", lambda e: e.activation(out=lnv, in_=ps[4 + par][:, 0:QT], func=ACT.Ln, scale=1.0 / 128, bias=EPS),
             reads=[("ps", 4 + par)], writes=["lnv"])
        p.op("act", lambda e: e.activation(out=rs, in_=lnv, func=ACT.Exp, scale=-0.5), reads=["lnv"], writes=["rs"])
        o_b = ob[par]
        p.op("dve", lambda e: e.scalar_tensor_tensor(o_b, t1[par], sgs, rs, ALU.mult, ALU.mult),
             reads=[("t1", par), "sgs", "rs"], writes=[("ob", par)])
        p.dma("sp", lambda e: e.dma_start(out=o_out[:, q0:q0 + QT], in_=o_b), ("ob", par), reads=[("ob", par)])

    N = len(steps)
    pend1, pend2 = [], []
    qk(0)
    for n in range(N):
        if n + 1 < N:
            qk(n + 1)
        softmax_pv(n)
        for (due, qt_) in list(pend2):
            if due <= n:
                fin2(qt_)
                pend2.remove((due, qt_))
        for (due, qt_) in list(pend1):
            if due <= n:
                fin1(qt_, n % 2)
                pend1.remove((due, qt_))
                pend2.append((n + 1, qt_))
        qt, kb = steps[n]
        if kb == 4 * qt + 3:
            pend1.append((n + 1, qt))
    for (due, qt_) in pend2:
        fin2(qt_)
    for (due, qt_) in pend1:
        fin1(qt_, N % 2)
        fin2(qt_)
    p.emit()
    es.close()
    return nc


def _b_inputs(H, rT, l, core):
    b, h = core // 4, core % 4
    I = H.inp
    e = l // 2
    rows = slice(h * 128, (h + 1) * 128)
    qT = np.concatenate([rT[b * 4 + j]["q_out"][rows, HALO:] for j in range(4)], axis=1)
    kT = np.concatenate([rT[b * 4 + j]["k_out"][rows, HALO:] for j in range(4)], axis=1)
    vT = np.concatenate([rT[b * 4 + j]["v_out"][rows, HALO:] for j in range(4)], axis=1)
    v = np.ascontiguousarray(vT.T.reshape(NKB, 128, 128).transpose(1, 0, 2).reshape(128, NKB * 128))
    lam = np.concatenate([np.asarray(I[n][e]) for n in ("lambda_q1", "lambda_k1", "lambda_q2", "lambda_k2")])
    lam = np.ascontiguousarray(np.broadcast_to(lam[None, :], (128, 256))).astype(np.float32)
    sg = np.ascontiguousarray(np.asarray(I["subln_g"][e]).reshape(128, 1))
    tri = (np.arange(128)[:, None] <= np.arange(128)[None, :]).astype(np.float32)
    return {"qT": np.ascontiguousarray(qT), "kT": np.ascontiguousarray(kT), "v": v, "lam": lam, "sg": sg, "tri": tri}


def _o_for(rB, core):
    b, j = core // 4, core % 4
    lo = j * TOK - HALO
    o = np.zeros((512, NTOK), dtype=rB[0]["oT"].dtype)
    for h in range(4):
        src = rB[b * 4 + h]["oT"][:, max(lo, 0):(j + 1) * TOK]
        o[h * 128:(h + 1) * 128, NTOK - src.shape[1]:] = src
    return o


def _run(nc, in_maps):
    return run_bass_kernel_spmd(nc, in_maps, core_ids=list(range(N_CORES))).results


def _assemble_mods(rA):
    mods = {}
    for b in range(BATCH):
        tabs = {l: np.zeros((128, 72), np.float32) for l in (1, 2, 3)}
        for i in range(27):
            j, pos = i % 4, i // 4
            tabs[1 + i // 9][:, (i % 9) * 8:(i % 9 + 1) * 8] = rA[b * 4 + j]["modx_out"][:, pos * 8:(pos + 1) * 8]
        for j in range(4):
            mods[b * 4 + j] = dict(mod0=rA[b * 4 + j]["mod0_out"], mod1=tabs[1], mod2=tabs[2], mod3=tabs[3])
    return mods


def kernel(**inputs):
    H = host_prep(inputs)
    cores = range(N_CORES)
    nc, nin, _ = build_T([("ffn", 0, 0), ("inproj", 0)], True)
    rA = _run(nc, [H.t_inputs(c, nin, x_full=inputs["x"]) for c in cores])
    mods = _assemble_mods(rA)
    rB = _run(build_B(0), [_b_inputs(H, rA, 0, c) for c in cores])
    carry = [dict(x_out=rA[c]["x_out"], o_in=_o_for(rB, c), p_in=rA[c]["p_out"], **mods[c]) for c in cores]
    nc, nin, _ = build_T([("mixout", 0), ("ffn", 0, 1), ("ffn", 1, 0), ("conv", 1), ("ffn", 1, 1),
                          ("ffn", 2, 0), ("inproj", 2)], False)
    rC = _run(nc, [H.t_inputs(c, nin, carry=carry) for c in cores])
    del rA, carry
    rB = _run(build_B(2), [_b_inputs(H, rC, 2, c) for c in cores])
    carry = [dict(x_out=rC[c]["x_out"], o_in=_o_for(rB, c), p_in=rC[c]["p_out"], **mods[c]) for c in cores]
    nc, nin, _ = build_T([("mixout", 2), ("ffn", 2, 1), ("ffn", 3, 0), ("conv", 3), ("ffn", 3, 1)], False)
    rD = _run(nc, [H.t_inputs(c, nin, carry=carry) for c in cores])
    out = np.empty((BATCH, SEQ, D_MODEL), np.float32)
    for c in cores:
        b, j = c // 4, c % 4
        out[b, j * TOK:(j + 1) * TOK] = rD[c]["x_out"][:, HALO:].T
    return out
```
